# Optimizing a Trainium2 kernel written in Bass

```python
import jax, jax.numpy as jnp
from jax import lax
import numpy as np

D_MODEL = 1024
BATCH = 8
SEQ = 2048
DEPTH = 4
DEC_BATCH = 128
DEC_SEQ = 8
PAST_LEN = 16384
PAGE_SIZE = 128

MIX_WIDTH = D_MODEL
GDN_HEADS = 4
GDN_DK = 128
GDN_DV = 128
GDN_QK = GDN_HEADS * GDN_DK
GDN_V = GDN_HEADS * GDN_DV
GDN_CONV = 4
GDN_CHUNK = 64
SC_WIDTH = MIX_WIDTH - GDN_V
SC_CONV = 3
MEM_LEN = 256
X_HEADS = 4
X_HEAD_DIM = D_MODEL // X_HEADS
D_FF = 2816
RMS_EPS = 1e-6
QKV_WIDTH = 2 * GDN_QK + GDN_V
OFF_Z = QKV_WIDTH
OFF_BETA = OFF_Z + GDN_V
OFF_A = OFF_BETA + GDN_HEADS
OFF_SC = OFF_A + GDN_HEADS
IN_COLS = OFF_SC + 3 * SC_WIDTH

kernel_name = "hymba_gdn_shortconv_macaron_memxattn_step"


def rmsnorm(x, g):
    xf = x.astype(jnp.float32)
    y = xf * lax.rsqrt(jnp.mean(xf * xf, axis=-1, keepdims=True) + RMS_EPS)
    return (y * g.astype(jnp.float32)).astype(x.dtype)


def l2norm(t):
    return t * lax.rsqrt(jnp.sum(t * t, axis=-1, keepdims=True) + 1e-6)


def swiglu(x, w_gu, w_down):
    gate, up = jnp.split(x @ w_gu, 2, axis=-1)
    return (jax.nn.silu(gate) * up) @ w_down


def causal_dwconv(u, buf, w):
    width = w.shape[0]
    ucat = jnp.concatenate([buf.astype(u.dtype), u], axis=1)
    out = lax.conv_general_dilated(ucat, w[:, None, :].astype(u.dtype), window_strides=(1,),
                                   padding='VALID', dimension_numbers=('NWC', 'WIO', 'NWC'),
                                   feature_group_count=u.shape[-1])
    return out, ucat[:, ucat.shape[1] - (width - 1):]


def gated_delta_chunked(q, k, v, g, beta, s0):
    bsz, L = q.shape[0], q.shape[1]
    C = min(GDN_CHUNK, L)
    n = -(-L // C)
    pad = n * C - L

    def to_chunks(t):
        t = jnp.pad(t, [(0, 0), (0, pad)] + [(0, 0)] * (t.ndim - 2))
        t = t.reshape((bsz, n, C) + t.shape[2:])
        return jnp.moveaxis(t, 2, 3)

    q, k, v, g, beta = (to_chunks(t) for t in (q, k, v, g, beta))
    gc = jnp.cumsum(g, axis=-1)
    tri = jnp.tril(jnp.ones((C, C), dtype=bool))
    decay = jnp.exp(jnp.where(tri, gc[..., :, None] - gc[..., None, :], -jnp.inf))
    kb = k * beta[..., None]
    vb = v * beta[..., None]
    eye = jnp.eye(C, dtype=jnp.float32)
    a_strict = jnp.einsum('bnhik,bnhjk->bnhij', kb, k) * decay * (1.0 - eye)
    t_inv = lax.linalg.triangular_solve(eye + a_strict, jnp.broadcast_to(eye, a_strict.shape),
                                        left_side=True, lower=True, unit_diagonal=True)
    u = jnp.einsum('bnhij,bnhjv->bnhiv', t_inv, vb)
    w = jnp.einsum('bnhij,bnhjk->bnhik', t_inv, kb * jnp.exp(gc)[..., None])
    qk = jnp.einsum('bnhik,bnhjk->bnhij', q, k) * decay
    q_dec = q * jnp.exp(gc)[..., None]
    g_last = gc[..., -1]
    k_tail = k * jnp.exp(g_last[..., None] - gc)[..., None]
    xs = tuple(jnp.moveaxis(t, 1, 0) for t in (u, w, qk, q_dec, k_tail, g_last))

    def step(S, xs_i):
        u_i, w_i, qk_i, qdec_i, ktail_i, glast_i = xs_i
        v_new = u_i - jnp.einsum('bhck,bhkv->bhcv', w_i, S)
        o = jnp.einsum('bhck,bhkv->bhcv', qdec_i, S) + jnp.einsum('bhij,bhjv->bhiv', qk_i, v_new)
        S = S * jnp.exp(glast_i)[..., None, None] + jnp.einsum('bhck,bhcv->bhkv', ktail_i, v_new)
        return S, o

    s_fin, o = lax.scan(step, s0, xs)
    o = jnp.transpose(o, (1, 0, 3, 2, 4)).reshape(bsz, n * C, GDN_HEADS, GDN_DV)[:, :L]
    return o, s_fin


def gdn_group(proj, qkv_buf, s0, conv_w, a_log, dt_bias, g_norm):
    bsz, L = proj.shape[0], proj.shape[1]
    qkv_c, new_buf = causal_dwconv(proj[..., :QKV_WIDTH], qkv_buf, conv_w)
    qkv_c = jax.nn.silu(qkv_c.astype(jnp.float32))
    q = qkv_c[..., :GDN_QK].reshape(bsz, L, GDN_HEADS, GDN_DK)
    k = qkv_c[..., GDN_QK:2 * GDN_QK].reshape(bsz, L, GDN_HEADS, GDN_DK)
    v = qkv_c[..., 2 * GDN_QK:].reshape(bsz, L, GDN_HEADS, GDN_DV)
    q = l2norm(q) * (GDN_DK ** -0.5)
    k = l2norm(k)
    beta = jax.nn.sigmoid(proj[..., OFF_BETA:OFF_A].astype(jnp.float32))
    g = -jnp.exp(a_log.astype(jnp.float32)) * jax.nn.softplus(
        proj[..., OFF_A:OFF_SC].astype(jnp.float32) + dt_bias.astype(jnp.float32))
    o, s_new = gated_delta_chunked(q, k, v, g, beta, s0.astype(jnp.float32))
    z = proj[..., OFF_Z:OFF_BETA].astype(jnp.float32).reshape(bsz, L, GDN_HEADS, GDN_DV)
    o = rmsnorm(o, g_norm) * jax.nn.silu(z)
    return o.reshape(bsz, L, GDN_V).astype(proj.dtype), new_buf, s_new


def short_conv_group(proj_sc, buf, w):
    b_gate, c_gate, h = jnp.split(proj_sc, 3, axis=-1)
    y, new_buf = causal_dwconv(c_gate * h, buf, w)
    return b_gate * y, new_buf


def mem_kv(mem, w_k, w_v):
    bsz = mem.shape[0]
    k = (mem @ w_k).reshape(bsz, MEM_LEN, X_HEADS, X_HEAD_DIM)
    v = (mem @ w_v).reshape(bsz, MEM_LEN, X_HEADS, X_HEAD_DIM)
    return k, v


def mem_attend(xn, k, v, w_q, w_o):
    bsz, L = xn.shape[0], xn.shape[1]
    q = (xn @ w_q).reshape(bsz, L, X_HEADS, X_HEAD_DIM)
    s = jnp.einsum('blhd,bmhd->bhlm', q, k.astype(q.dtype)).astype(jnp.float32) * (X_HEAD_DIM ** -0.5)
    p = jax.nn.softmax(s, axis=-1).astype(q.dtype)
    o = jnp.einsum('bhlm,bmhd->blhd', p, v.astype(q.dtype)).reshape(bsz, L, X_HEADS * X_HEAD_DIM)
    return o @ w_o


def trunk_layer(x, mk, mv, qkv_buf, sc_buf, s0, lw):
    x = x + 0.5 * swiglu(rmsnorm(x, lw['g_ffn1']), lw['w_ffn1_gu'], lw['w_ffn1_down'])
    proj = rmsnorm(x, lw['g_mix']) @ lw['w_in']
    o_gdn, qkv_buf, s_new = gdn_group(proj, qkv_buf, s0, lw['conv_qkv_w'], lw['a_log'],
                                      lw['dt_bias'], lw['g_gdn_out'])
    o_sc, sc_buf = short_conv_group(proj[..., OFF_SC:], sc_buf, lw['sconv_w'])
    x = x + jnp.concatenate([o_gdn, o_sc], axis=-1) @ lw['w_out']
    x = x + mem_attend(rmsnorm(x, lw['g_xattn']), mk, mv, lw['w_xq'], lw['w_xo'])
    x = x + 0.5 * swiglu(rmsnorm(x, lw['g_ffn2']), lw['w_ffn2_gu'], lw['w_ffn2_down'])
    return x, qkv_buf, sc_buf, s_new


def setup_inputs(seed: int = 0) -> dict:
    key = jax.random.key(seed)
    ks = jax.random.split(key, 32)
    f32 = jnp.float32

    def nrm(k, shape, scale):
        return jax.random.normal(k, shape, f32) * scale

    def gain(k, shape):
        return 1.0 + 0.02 * jax.random.normal(k, shape, f32)

    a_log = jnp.log(jax.random.uniform(ks[13], (DEPTH, GDN_HEADS), f32, 1.0, 16.0))
    dt = jnp.exp(jax.random.uniform(ks[14], (DEPTH, GDN_HEADS), f32, np.log(1e-3), np.log(1e-1)))
    dt_bias = dt + jnp.log(-jnp.expm1(-dt))
    return {
        "x_prompt": nrm(ks[0], (BATCH, SEQ, D_MODEL), 1.0),
        "x_sample": nrm(ks[1], (DEC_BATCH, DEC_SEQ, D_MODEL), 1.0),
        "mem_prompt": nrm(ks[2], (BATCH, MEM_LEN, D_MODEL), 1.0),
        "state_gdn": nrm(ks[3], (DEPTH, DEC_BATCH, GDN_HEADS, GDN_DK, GDN_DV), GDN_DK ** -0.5),
        "state_qkv_conv": nrm(ks[4], (DEPTH, DEC_BATCH, GDN_CONV - 1, QKV_WIDTH), 1.0),
        "state_short_conv": nrm(ks[5], (DEPTH, DEC_BATCH, SC_CONV - 1, SC_WIDTH), 1.0),
        "cache_mem_k": nrm(ks[6], (DEPTH, DEC_BATCH, MEM_LEN, X_HEADS, X_HEAD_DIM), 1.0),
        "cache_mem_v": nrm(ks[7], (DEPTH, DEC_BATCH, MEM_LEN, X_HEADS, X_HEAD_DIM), 1.0),
        "g_ffn1": gain(ks[8], (DEPTH, D_MODEL)),
        "w_ffn1_gu": nrm(ks[9], (DEPTH, D_MODEL, 2 * D_FF), D_MODEL ** -0.5),
        "w_ffn1_down": nrm(ks[10], (DEPTH, D_FF, D_MODEL), D_FF ** -0.5),
        "g_mix": gain(ks[11], (DEPTH, D_MODEL)),
        "w_in": nrm(ks[12], (DEPTH, D_MODEL, IN_COLS), D_MODEL ** -0.5),
        "conv_qkv_w": nrm(ks[15], (DEPTH, GDN_CONV, QKV_WIDTH), GDN_CONV ** -0.5),
        "a_log": a_log,
        "dt_bias": dt_bias,
        "g_gdn_out": gain(ks[16], (DEPTH, GDN_DV)),
        "sconv_w": nrm(ks[17], (DEPTH, SC_CONV, SC_WIDTH), SC_CONV ** -0.5),
        "w_out": nrm(ks[18], (DEPTH, MIX_WIDTH, D_MODEL), MIX_WIDTH ** -0.5),
        "g_xattn": gain(ks[19], (DEPTH, D_MODEL)),
        "w_xq": nrm(ks[20], (DEPTH, D_MODEL, X_HEADS * X_HEAD_DIM), D_MODEL ** -0.5),
        "w_xk": nrm(ks[21], (DEPTH, D_MODEL, X_HEADS * X_HEAD_DIM), D_MODEL ** -0.5),
        "w_xv": nrm(ks[22], (DEPTH, D_MODEL, X_HEADS * X_HEAD_DIM), D_MODEL ** -0.5),
        "w_xo": nrm(ks[23], (DEPTH, X_HEADS * X_HEAD_DIM, D_MODEL), D_MODEL ** -0.5),
        "g_ffn2": gain(ks[24], (DEPTH, D_MODEL)),
        "w_ffn2_gu": nrm(ks[25], (DEPTH, D_MODEL, 2 * D_FF), D_MODEL ** -0.5),
        "w_ffn2_down": nrm(ks[26], (DEPTH, D_FF, D_MODEL), D_FF ** -0.5),
        "g_final": gain(ks[27], (D_MODEL,)),
    }


def reference(x_prompt, x_sample, mem_prompt, state_gdn, state_qkv_conv, state_short_conv,
              cache_mem_k, cache_mem_v, g_ffn1, w_ffn1_gu, w_ffn1_down, g_mix, w_in, conv_qkv_w,
              a_log, dt_bias, g_gdn_out, sconv_w, w_out, g_xattn, w_xq, w_xk, w_xv, w_xo,
              g_ffn2, w_ffn2_gu, w_ffn2_down, g_final):
    yp, ys = x_prompt, x_sample
    p_s, p_qb, p_sb, p_mk, p_mv = [], [], [], [], []
    s_s, s_qb, s_sb = [], [], []
    for l in range(DEPTH):
        lw = {
            'g_ffn1': g_ffn1[l], 'w_ffn1_gu': w_ffn1_gu[l], 'w_ffn1_down': w_ffn1_down[l],
            'g_mix': g_mix[l], 'w_in': w_in[l], 'conv_qkv_w': conv_qkv_w[l], 'a_log': a_log[l],
            'dt_bias': dt_bias[l], 'g_gdn_out': g_gdn_out[l], 'sconv_w': sconv_w[l],
            'w_out': w_out[l], 'g_xattn': g_xattn[l], 'w_xq': w_xq[l], 'w_xo': w_xo[l],
            'g_ffn2': g_ffn2[l], 'w_ffn2_gu': w_ffn2_gu[l], 'w_ffn2_down': w_ffn2_down[l],
        }
        mk, mv = mem_kv(mem_prompt, w_xk[l], w_xv[l])
        yp, qb, sb, st = trunk_layer(
            yp, mk, mv,
            jnp.zeros((BATCH, GDN_CONV - 1, QKV_WIDTH), yp.dtype),
            jnp.zeros((BATCH, SC_CONV - 1, SC_WIDTH), yp.dtype),
            jnp.zeros((BATCH, GDN_HEADS, GDN_DK, GDN_DV), jnp.float32), lw)
        p_s.append(st); p_qb.append(qb); p_sb.append(sb); p_mk.append(mk); p_mv.append(mv)
        ys, qb, sb, st = trunk_layer(ys, cache_mem_k[l], cache_mem_v[l], state_qkv_conv[l],
                                     state_short_conv[l], state_gdn[l], lw)
        s_s.append(st); s_qb.append(qb); s_sb.append(sb)
    y_prompt = rmsnorm(yp, g_final)
    y_sample = rmsnorm(ys, g_final)
    return (y_prompt, y_sample,
            jnp.stack(p_s), jnp.stack(p_qb), jnp.stack(p_sb), jnp.stack(p_mk), jnp.stack(p_mv),
            jnp.stack(s_s), jnp.stack(s_qb), jnp.stack(s_sb))
```

```python
import numpy as np
from contextlib import ExitStack
import concourse.bass as bass
import concourse.mybir as mybir
from concourse.bass_utils import run_bass_kernel_spmd

F32 = mybir.dt.float32
BF16 = mybir.dt.bfloat16
AF = mybir.ActivationFunctionType
ALU = mybir.AluOpType

ENGINES = ("tensor", "vector", "scalar", "gpsimd", "sync")

D = 1024
DFF = 2816
NFC = DFF // 128
INC = 3592
OFF_Z, OFF_BETA, OFF_A, OFF_SC = 1536, 2048, 2052, 2056
MEM = 256
NSEQ = 16
LS = 8
NEG = -30000.0


class Sem:
    def __init__(self, name):
        self.name = name
        self.total = 0
        self.handle = None
        self.is_dma = name.startswith("D_")


class Buf:
    __slots__ = ("name", "w", "r", "excl")

    def __init__(self, name="", excl=False):
        self.name = name
        self.w = None
        self.r = {}
        self.excl = excl


class Prog:
    def __init__(self, dry=False):
        self.dry = dry
        self.ops = {e: [] for e in ENGINES}
        self.esem = {e: Sem("E_" + e) for e in ENGINES}
        self.sems = list(self.esem.values())
        self.seen = {e: {} for e in ENGINES}

    def new_sem(self, name):
        s = Sem(name)
        self.sems.append(s)
        return s

    def _waits(self, eng, reads, writes, extra=()):
        need = {}
        for b in reads:
            if b.w is not None:
                s, v = b.w
                if need.get(s, 0) < v:
                    need[s] = v
        for b in writes:
            if b.w is not None:
                s, v = b.w
                if need.get(s, 0) < v:
                    need[s] = v
            for s, v in b.r.items():
                if need.get(s, 0) < v:
                    need[s] = v
        for s, v in extra:
            if need.get(s, 0) < v:
                need[s] = v
        out = []
        seen = self.seen[eng]
        pe = self.esem["tensor"]
        for s, v in need.items():
            if eng == "tensor" and s is pe:
                continue
            if s.is_dma:
                v = s.total
            if seen.get(s, 0) >= v:
                continue
            seen[s] = v
            out.append((s, v))
        return out

    def _mark(self, t, reads, writes):
        s, v = t
        for b in reads:
            if b.r.get(s, 0) < v:
                b.r[s] = v
        for b in writes:
            b.w = t
            b.r = {}

    def op(self, eng, fn, reads=(), writes=()):
        if self.dry:
            return None
        ex = [b for b in reads if b.excl]
        if ex:
            writes = list(writes) + ex
        waits = self._waits(eng, reads, writes)
        s = self.esem[eng]
        s.total += 1
        t = (s, s.total)
        self.ops[eng].append((fn, waits, (s, 1)))
        self._mark(t, reads, writes)
        return t

    def dma(self, eng, sem, fn, reads=(), writes=()):
        if self.dry:
            return None
        waits = self._waits(eng, reads, writes)
        sem.total += 16
        t = (sem, sem.total)
        self.ops[eng].append((fn, waits, (sem, 16)))
        self._mark(t, reads, writes)
        return t

    def wait(self, eng, tickets):
        waits = self._waits(eng, (), (), tickets)
        if waits:
            self.ops[eng].append((None, waits, None))

    def replay(self, block):
        def run(e):
            def body(h):
                for fn, waits, inc in self.ops[e]:
                    for s, v in waits:
                        h.wait_ge(s.handle, v)
                    if fn is None:
                        continue
                    ins = fn(h)
                    if inc is not None:
                        ins.then_inc(inc[0].handle, inc[1])
            return body
        block.tensor(run("tensor"))
        block.vector(run("vector"))
        block.scalar(run("scalar"))
        block.gpsimd(run("gpsimd"))
        block.sync(run("sync"))


class Ring:
    def __init__(self, tiles):
        self.tiles = tiles
        self.bufs = [Buf() for _ in tiles]
        self.i = 0

    def next(self):
        k = self.i % len(self.tiles)
        self.i += 1
        return self.tiles[k], self.bufs[k]


def make_blocks(seq):
    if seq == 2048:
        gl = [[(0, 512, 'p'), (512, 256, 'p')], [(768, 256, 'p'), (1024, 512, 'p')],
              [(1536, 512, 'p'), (2048, 128, 's')]]
    else:
        assert seq % 128 == 0 and seq <= 512
        gl = [[(0, seq, 'p')], [(seq, 128, 's')]] if seq <= 128 else \
             [[(0, seq // 2, 'p')], [(seq // 2, seq // 2, 'p'), (seq, 128, 's')]]
    blocks = []
    for g in gl:
        lo = 0
        grp = []
        for (t0, n, kind) in g:
            grp.append((t0, n, kind, lo))
            lo += n
        blocks.append(grp)
    return blocks


class Builder:
    def __init__(self, nc, L, SEQ, dry, log=None):
        self.nc, self.L, self.SEQ = nc, L, SEQ
        self.NT = SEQ + 128
        self.blocks = make_blocks(SEQ)
        self.TBM = max(sum(g[1] for g in b) for b in self.blocks)
        self.NTM = self.TBM // 128
        self.P = Prog(dry)
        self.dry = dry
        self.slab_log = [] if log is None else log
        self.slab_i = 0
        self.slab_emitted = 0
        self.kv_log_i = 0

    def mm(self, out, lhsT, rhs, start, stop, r, w):
        self.P.op("tensor", lambda h: h.matmul(out, lhsT=lhsT, rhs=rhs, start=start, stop=stop), r, w)

    def tr(self, out, in_, ident, r, w):
        self.P.op("tensor", lambda h: h.transpose(out=out, in_=in_, identity=ident), r, w)

    def act(self, out, in_, func, r, w, scale=None, bias=None):
        kw = {}
        if scale is not None:
            kw["scale"] = scale
        if bias is not None:
            kw["bias"] = bias
        self.P.op("scalar", lambda h: h.activation(out=out, in_=in_, func=func, **kw), r, w)

    def tt(self, eng, out, in0, in1, op, r, w):
        self.P.op(eng, lambda h: h.tensor_tensor(out=out, in0=in0, in1=in1, op=op), r, w)

    def ts(self, eng, out, in0, s1, op0, r, w, s2=None, op1=None):
        if op1 is None:
            self.P.op(eng, lambda h: h.tensor_scalar(out=out, in0=in0, scalar1=s1, scalar2=None, op0=op0), r, w)
        else:
            self.P.op(eng, lambda h: h.tensor_scalar(out=out, in0=in0, scalar1=s1, scalar2=s2, op0=op0, op1=op1), r, w)

    def stt(self, out, in0, scalar, in1, op0, op1, r, w):
        self.P.op("vector", lambda h: h.scalar_tensor_tensor(out=out, in0=in0, scalar=scalar, in1=in1,
                                                             op0=op0, op1=op1), r, w)

    def cp(self, eng, out, in_, r, w):
        if eng == "scalar":
            self.P.op("scalar", lambda h: h.activation(out=out, in_=in_, func=AF.Copy), r, w)
        else:
            self.P.op(eng, lambda h: h.tensor_copy(out=out, in_=in_), r, w)

    def recip(self, out, in_, r, w):
        self.P.op("vector", lambda h: h.reciprocal(out=out, in_=in_), r, w)

    def memset(self, eng, out, val, r, w):
        self.P.op(eng, lambda h: h.memset(out, val), r, w)

    def dma(self, q, sem, out, in_, r, w):
        return self.P.dma(q, sem, lambda h: h.dma_start(out=out, in_=in_), r, w)

    def sb(self, name, shape, dt):
        return self.es.enter_context(self.nc.sbuf_tensor(name, list(shape), dt))

    def ring_sb(self, name, shape, dt, n):
        return Ring([self.sb(f"{name}{i}", shape, dt) for i in range(n)])

    def slab(self, wname, l, k0, KC, ranges, live=1):
        ranges = tuple(ranges)
        ncol = sum(n for _, n in ranges)
        assert KC * ncol <= self.SLABE
        if self.dry:
            self.slab_log.append((wname, l, k0, KC, ranges))
            return self.slab_tiles[0][:, :KC * ncol].rearrange("p (k n) -> p k n", n=ncol), self.slab_bufs[0]
        i = self.slab_i
        self.slab_i += 1
        oldest_live = i - (live - 1)
        lim = min(len(self.slab_log), i + 1 + self.PF)
        while self.slab_emitted < lim and (self.slab_emitted <= i or self.slab_emitted - self.NSLAB < oldest_live):
            j = self.slab_emitted
            assert j - self.NSLAB < oldest_live, "slab ring too small for live set"
            wn2, l2, k02, kc2, rg2 = self.slab_log[j]
            n2 = sum(n for _, n in rg2)
            k = j % self.NSLAB
            dst = self.slab_tiles[k][:, :kc2 * n2].rearrange("p (k n) -> p k n", n=n2)
            src = self.W[wn2][l2].rearrange("(k p) n -> p k n", p=128)
            o = 0
            for (c2, nn) in rg2:
                self.dma("gpsimd", self.slab_sems[k], dst[:, :, o:o + nn], src[:, k02:k02 + kc2, c2:c2 + nn], [], [self.slab_bufs[k]])
                o += nn
            self.slab_emitted += 1
        k = i % self.NSLAB
        assert self.slab_log[i] == (wname, l, k0, KC, ranges), (self.slab_log[i], wname, l, k0, KC, ranges)
        return self.slab_tiles[k][:, :KC * ncol].rearrange("p (k n) -> p k n", n=ncol), self.slab_bufs[k]

    def build(self):
        nc, L, SEQ, NT, TBM, NTM = self.nc, self.L, self.SEQ, self.NT, self.TBM, self.NTM
        P = self.P
        dram = lambda n, s, k: nc.dram_tensor(n, list(s), F32, kind=k).ap()
        I, O = "ExternalInput", "ExternalOutput"
        self.xp = dram("xp", [SEQ, D], I)
        self.xs = dram("xs", [128, D], I)
        self.mem = dram("mem", [MEM, D], I)
        self.sgdn = dram("sgdn", [L, NSEQ, 4, 128, 128], I)
        self.sqkv = dram("sqkv", [L, NSEQ * 3, 1536], I)
        self.ssc = dram("ssc", [L, NSEQ * 2, 512], I)
        self.cK = dram("cK", [L, NSEQ, MEM, D], I)
        self.cV = dram("cV", [L, NSEQ, MEM, D], I)
        self.W = {"gu1": dram("w_ffn1_gu", [L, D, 2 * DFF], I), "gu2": dram("w_ffn2_gu", [L, D, 2 * DFF], I),
                  "dn1": dram("w_ffn1_down", [L, DFF, D], I), "dn2": dram("w_ffn2_down", [L, DFF, D], I),
                  "in": dram("w_in", [L, D, INC], I), "out": dram("w_out", [L, D, D], I),
                  "xq": dram("w_xq", [L, D, D], I), "xk": dram("w_xk", [L, D, D], I),
                  "xv": dram("w_xv", [L, D, D], I), "xo": dram("w_xo", [L, D, D], I)}
        self.NV = L * 32 + 8 + L * 48 + L * 12 + L + 4
        self.cvec_d = dram("cvec", [128, self.NV], I)
        self.crow_d = dram("crow", [128, 2 * L * 6 * 4], I)
        self.cmat_d = dram("cmat", [128, 9 * 128 + 16], I)
        self.yp = dram("yp", [SEQ, D], O)
        self.ys = dram("ys", [128, D], O)
        self.o_sp = dram("o_sp", [L, 4, 128, 128], O)
        self.o_qp = dram("o_qp", [L, 3 * 12, 128], O)
        self.o_cp = dram("o_cp", [L, 2 * 4, 128], O)
        self.o_mk = dram("o_mk", [L, MEM, D], O)
        self.o_mv = dram("o_mv", [L, MEM, D], O)
        self.o_ss = dram("o_ss", [L, NSEQ, 4, 128, 128], O)
        self.o_qs = dram("o_qs", [L, NSEQ * 3, 1536], O)
        self.o_cs = dram("o_cs", [L, NSEQ * 2, 512], O)

        with ExitStack() as es:
            self.es = es
            sb = self.sb
            self.xT = sb("xT", [128, 8, NT], F32)
            self.XT = {}
            for b in self.blocks:
                for (t0, n, kind, lo) in b:
                    self.XT[t0] = Buf("xT%d" % t0)
            self.xn = sb("xn", [128, 8, TBM], BF16)
            self.XN = [Buf("xn%d" % i) for i in range(4)]
            self.cvec = sb("cvec_s", [128, self.NV], F32)
            self.crow = sb("crow_s", [128, 2 * L * 6 * 4], F32)
            self.cmat = sb("cmat_s", [128, 9 * 128 + 16], F32)
            self.CONST = Buf("const")
            self.identb = sb("identb", [128, 128], BF16)
            self.onesb = sb("onesb", [128, 128], BF16)
            self.nega = sb("nega", [128, L * 6 * 4], F32)
            self.memT = sb("memT", [128, 8, MEM], BF16)
            self.MEMT = Buf("memT")
            OVB = max(NFC * TBM * 2, 1)
            mixb = (4 + 16) * TBM * 2
            attb = (16 * TBM + 4096) * 2
            self.ovl = sb("ovl", [128, max(OVB, mixb, attb) // 2], BF16)
            self.ov_cur = []
            self.so_cur = []
            self.NSLAB, self.PF, self.SLABE = 4, 3, 2048
            self.slab_tiles = [sb(f"slab{i}", [128, self.SLABE], BF16) for i in range(self.NSLAB)]
            self.slab_bufs = [Buf() for _ in range(self.NSLAB)]
            self.slab_sems = [P.new_sem(f"D_slab{i}") for i in range(self.NSLAB)]
            self.r_f512 = self.ring_sb("f512_", [128, 512], F32, 4)
            self.r_ub = self.ring_sb("ub", [128, 520], BF16, 2)
            self.r_dw = self.ring_sb("dw", [128, 4, 128], BF16, 2)
            self.r_b512 = self.ring_sb("b512_", [128, 512], BF16, 4)
            self.r_f128 = self.ring_sb("f128_", [128, 128], F32, 4)
            self.r_b128 = self.ring_sb("b128_", [128, 128], BF16, 28)
            self.r_stage = self.ring_sb("stage", [128, 1024], F32, 1)
            self.stage_sems = [P.new_sem(f"D_stage{i}") for i in range(1)]
            self.hb = []
            for h in range(4):
                d_ = {}
                for nm in ("eg", "oi"):
                    d_[nm] = (sb(f"h{h}{nm}", [128, 128], F32), Buf(f"h{h}{nm}"))
                for nm in ("qkd", "qdec", "kbg", "ktl", "vb", "nw", "vn"):
                    d_[nm] = (sb(f"h{h}{nm}", [128, 128], BF16), Buf(f"h{h}{nm}"))
                self.hb.append(d_)
            self.r_stage.sems = self.stage_sems
            banks = [es.enter_context(nc.psum_tensor(f"pb{i}", [128, 512], F32)) for i in range(8)]
            self.r_big = Ring(banks)
            for b_ in self.r_big.bufs:
                b_.excl = True
            self.r_big4 = Ring(banks[0:4])
            self.r_big4.bufs = self.r_big.bufs[0:4]
            smalls, sbufs = [], []
            for i in range(4, 8):
                for q in range(4):
                    smalls.append(banks[i][:, q * 128:(q + 1) * 128])
                    sbufs.append(self.r_big.bufs[i])
            self.r_small = Ring(smalls)
            self.r_small.bufs = sbufs
            self.hps = []
            for h in range(4):
                r_ = Ring(smalls[4 * h:4 * h + 4])
                r_.bufs = sbufs[4 * h:4 * h + 4]
                self.hps.append(r_)
            self.S = [sb(f"S{h}", [128, 128], F32) for h in range(4)]
            self.Sb = [sb(f"Sb{h}", [128, 128], BF16) for h in range(4)]
            self.SB = [Buf(f"S{h}") for h in range(4)]
            self.carq = sb("carq", [128, 3, 12], F32)
            self.CARQ = [Buf() for _ in range(12)]
            self.carc = sb("carc", [128, 2, 4], F32)
            self.CARC = [Buf() for _ in range(4)]
            self.stq = sb("stq", [128, 12, NSEQ, 3], F32)
            self.STQ = [Buf() for _ in range(12)]
            self.stc = sb("stc", [128, 4, NSEQ, 2], F32)
            self.STC = [Buf() for _ in range(4)]
            self.gcols = sb("gcols", [128, 6, NTM, 4], F32)
            self.GCOL = Buf("gcols")
            self.sovl = sb("sovl", [128, 10240], BF16)
            so = self.sovl
            self.ssamp = so[:, 0:4096].bitcast(F32).rearrange("p (s v) -> p s v", v=128)
            self.ssampb = so[:, 4096:6144].rearrange("p (s v) -> p s v", v=128)
            self.wz = so[:, 6144:8192].rearrange("p (s v) -> p s v", v=128)
            self.ktz = so[:, 8192:10240].rearrange("p (s v) -> p s v", v=128)
            self.kvs_t = [so[:, i * 4096:(i + 1) * 4096].rearrange("p (a t d) -> p a t d", a=2, t=2) for i in range(2)]
            self.kvs_sems = [P.new_sem(f"D_kvs{i}") for i in range(2)]
            self.kts_t = so[:, 8192:10240].rearrange("p (c m) -> p c m", m=MEM)
            self.KT = self.ovl[:, 16 * TBM:16 * TBM + 2048].rearrange("p (c m) -> p c m", m=MEM)
            self.Vb = self.ovl[:, 16 * TBM + 2048:16 * TBM + 4096].rearrange("p (t d) -> p t d", d=D)
            self.sem_in = P.new_sem("D_in")
            self.sem_out = P.new_sem("D_out")
            self.sem_ss = P.new_sem("D_ss")
            for s in P.sems:
                s.handle = es.enter_context(nc.semaphore(s.name))

            self.emit()

            if not self.dry:
                P.wait("sync", [(s, s.total) for s in P.sems if s.is_dma and s.total > 0])
                block = es.enter_context(nc.Block())
                P.replay(block)
        return nc

    def gain(self, l, which, c):
        o = (l * 4 + which) * 8 + c
        return self.cvec[:, o:o + 1]

    def gfin(self, c):
        o = self.L * 32 + c
        return self.cvec[:, o:o + 1]

    def cqw(self, l, c, j):
        o = self.L * 32 + 8 + (l * 12 + c) * 4 + j
        return self.cvec[:, o:o + 1]

    def scw(self, l, c, j):
        o = self.L * 32 + 8 + self.L * 48 + (l * 4 + c) * 3 + j
        return self.cvec[:, o:o + 1]

    def ggdn(self, l):
        o = self.L * 32 + 8 + self.L * 48 + self.L * 12 + l
        return self.cvec[:, o:o + 1]

    def cst(self, k):
        o = self.L * 32 + 8 + self.L * 48 + self.L * 12 + self.L + k
        return self.cvec[:, o:o + 1]

    def cm(self, k):
        return self.cmat[:, k * 128:(k + 1) * 128]

    def emit(self):
        L = self.L
        C = self.CONST
        q = "sync"
        self.dma(q, self.sem_in, self.cvec[:], self.cvec_d[:, :], [], [C])
        self.dma(q, self.sem_in, self.crow[:], self.crow_d[:, :], [], [C])
        self.dma(q, self.sem_in, self.cmat[:], self.cmat_d[:, :], [], [C])
        self.cp("vector", self.identb[:], self.cm(0), [C], [C])
        self.memset("vector", self.onesb[:], 1.0, [], [C])
        n24 = L * 24
        self.act(self.nega[:], self.crow[:, n24:2 * n24], AF.Exp, [C], [C])
        self.ts("vector", self.nega[:], self.nega[:], -1.0, ALU.mult, [C], [C])
        self.load_x()
        self.load_mem()
        import os
        ks = os.environ.get("KSTOP", "full")
        for l in range(L):
            for bi, blk in enumerate(self.blocks):
                if ks == "none":
                    continue
                self.ffn(l, 1, blk)
                if ks == "ffn":
                    continue
                self.mix(l, bi, blk)
                if ks.startswith("mix"):
                    continue
                self.attn(l, bi, blk)
                self.ffn(l, 2, blk)
        self.final()

    def load_x(self):
        C = self.CONST
        ident = self.cm(0)
        ntile = self.NT // 128
        for ti in range(ntile):
            st, SB_ = self.r_stage.next()
            k = 0
            src = self.xp[ti * 128:(ti + 1) * 128, :] if ti * 128 < self.SEQ else self.xs[:, :]
            self.dma("sync", self.stage_sems[k], st[:], src, [], [SB_])
            t0 = ti * 128
            g0 = max(k_ for k_ in self.XT if k_ <= t0)
            for half in range(2):
                ps, PB = self.r_big.next()
                for c4 in range(4):
                    c = half * 4 + c4
                    self.tr(ps[:, c4 * 128:(c4 + 1) * 128], st[:, c * 128:(c + 1) * 128], ident, [SB_, C], [PB])
                self.cp("vector" if half else "scalar",
                        self.xT[:, half * 4:half * 4 + 4, t0:t0 + 128],
                        ps[:].rearrange("p (c n) -> p c n", n=128), [PB], [self.XT[g0]])

    def load_mem(self):
        C = self.CONST
        ident = self.cm(0)
        for mt in range(2):
            st, SB_ = self.r_stage.next()
            k = 0
            self.dma("sync", self.stage_sems[k], st[:], self.mem[mt * 128:(mt + 1) * 128, :], [], [SB_])
            for half in range(2):
                ps, PB = self.r_big.next()
                for c4 in range(4):
                    c = half * 4 + c4
                    self.tr(ps[:, c4 * 128:(c4 + 1) * 128], st[:, c * 128:(c + 1) * 128], ident, [SB_, C], [PB])
                self.cp("vector", self.memT[:, half * 4:half * 4 + 4, mt * 128:(mt + 1) * 128],
                        ps[:].rearrange("p (c n) -> p c n", n=128), [PB], [self.MEMT])

    def sumsq(self, X, t0, n):
        ps, PB = self.r_big.next()
        for c in range(8):
            sq, SQ = self.r_b512.next()
            self.act(sq[:, :n], self.xT[:, c, t0:t0 + n], AF.Square, [X], [SQ])
            self.mm(ps[:, :n], self.onesb[:], sq[:, :n], c == 0, c == 7, [SQ, self.CONST], [PB])
        rs, RS = self.r_f512.next()
        self.act(rs[:, :n], ps[:, :n], AF.Ln, [PB, self.CONST], [RS], scale=1.0 / D, bias=self.cst(0))
        self.act(rs[:, :n], rs[:, :n], AF.Exp, [RS], [RS], scale=-0.5)
        return rs, RS

    def norm(self, l, which, blk):
        for gi, (t0, n, kind, lo) in enumerate(blk):
            X = self.XT[t0]
            rs, RS = self.sumsq(X, t0, n)
            for c in range(8):
                self.stt(self.xn[:, c, lo:lo + n], self.xT[:, c, t0:t0 + n], self.gain(l, which, c), rs[:, :n],
                         ALU.mult, ALU.mult, [X, RS, self.CONST], [self.XN[gi]])

    def proj(self, sl, SL, KC, col, xin, XIN_g, gi, lo, n):
        ps, PB = self.r_big.next()
        for kc in range(KC):
            self.mm(ps[:, :n], sl[:, kc, col:col + 128], xin(kc, lo, n), kc == 0, kc == KC - 1, [SL, XIN_g], [PB])
        return ps, PB

    def xn_in(self, kc, lo, n):
        return self.xn[:, kc, lo:lo + n]

    @staticmethod
    def rr(gens):
        gens = list(gens)
        while gens:
            for g_ in list(gens):
                try:
                    next(g_)
                except StopIteration:
                    gens.remove(g_)

    def phase(self, names):
        merged = {}
        for b in self.ov_cur:
            if b.w is not None:
                s_, v = b.w
                merged[s_] = max(merged.get(s_, 0), v)
            for s_, v in b.r.items():
                merged[s_] = max(merged.get(s_, 0), v)
        new = []
        for nm in names:
            b = Buf(nm)
            b.r = dict(merged)
            new.append(b)
        self.ov_cur = new
        return new

    def ffn(self, l, which, blk):
        self.norm(l, 0 if which == 1 else 3, blk)
        TBM = self.TBM
        hT = self.ovl[:, :NFC * TBM].rearrange("p (c n) -> p c n", n=TBM)
        HB = self.phase(["h%d" % i for i in range(len(blk))])
        gu, dn = "gu%d" % which, "dn%d" % which
        for fp in range(NFC // 2):
            gsl, GSL = self.slab(gu, l, 0, 8, [(fp * 256, 256)])
            usl, USL = self.slab(gu, l, 0, 8, [(DFF + fp * 256, 256)], live=2)
            for i in range(2):
                fc = fp * 2 + i
                for gi, (t0, n, kind, lo) in enumerate(blk):
                    pg, PG = self.proj(gsl, GSL, 8, i * 128, self.xn_in, self.XN[gi], gi, lo, n)
                    pu, PU = self.proj(usl, USL, 8, i * 128, self.xn_in, self.XN[gi], gi, lo, n)
                    sg, SG = self.r_f512.next()
                    self.act(sg[:, :n], pg[:, :n], AF.Silu, [PG], [SG])
                    self.tt("vector", hT[:, fc, lo:lo + n], sg[:, :n], pu[:, :n], ALU.mult, [SG, PU], [HB[gi]])
        fparts = [(0, 8), (8, 8), (16, NFC - 16)]
        for dp in range(4):
            accs = [[self.r_big.next() for _ in blk] for _ in range(2)]
            for pi, (f0, nf) in enumerate(fparts):
                dsl, DSL = self.slab(dn, l, f0, nf, [(dp * 256, 256)])
                for i in range(2):
                    for gi, (t0, n, kind, lo) in enumerate(blk):
                        py, PY = accs[i][gi]
                        for kc in range(nf):
                            self.mm(py[:, :n], dsl[:, kc, i * 128:(i + 1) * 128], hT[:, f0 + kc, lo:lo + n],
                                    pi == 0 and kc == 0, pi == 2 and kc == nf - 1, [DSL, HB[gi]], [PY])
            for i in range(2):
                dc = dp * 2 + i
                for gi, (t0, n, kind, lo) in enumerate(blk):
                    py, PY = accs[i][gi]
                    X = self.XT[t0]
                    self.stt(self.xT[:, dc, t0:t0 + n], py[:, :n], 0.5, self.xT[:, dc, t0:t0 + n], ALU.mult, ALU.add,
                             [PY, X], [X])

    def mix(self, l, bi, blk):
        TBM, L = self.TBM, self.L
        C = self.CONST
        self.norm(l, 1, blk)
        ntile = sum(g[1] for g in blk) // 128
        has_s = blk[-1][2] == 's'
        last_p = (bi == len(self.blocks) - 1)
        ov = self.ovl
        osc = ov[:, 0:4 * TBM].rearrange("p (c n) -> p c n", n=TBM)
        qkvz = ov[:, 4 * TBM:20 * TBM].rearrange("p (c n) -> p c n", n=TBM)
        OSC, QKVZ = self.phase(["osc", "qkvz"])
        if bi == 0:
            for c in range(12):
                self.memset("vector", self.carq[:, :, c], 0.0, [], [self.CARQ[c]])
            for c in range(4):
                self.memset("vector", self.carc[:, :, c], 0.0, [], [self.CARC[c]])
            for h in range(4):
                self.memset("vector", self.S[h][:], 0.0, [], [self.SB[h]])
                self.memset("vector", self.Sb[h][:], 0.0, [], [self.SB[h]])
        if has_s:
            self.load_sample_conv_state(l)
        import os
        ks = os.environ.get("KSTOP", "full")
        if ks == "mix_0":
            return
        self.gates(l, blk, ntile, has_s)
        if ks == "mix_a":
            return
        for c in range(4):
            if c % 2 == 0:
                bsl, BSL = self.slab("in", l, 0, 8, [(OFF_SC + c * 128, 256)])
                csl, CSL = self.slab("in", l, 0, 8, [(OFF_SC + 512 + c * 128, 256)], live=2)
                hsl, HSL = self.slab("in", l, 0, 8, [(OFF_SC + 1024 + c * 128, 256)], live=3)
            co = (c % 2) * 128
            dw, DW = self.diagw(l, c, 3, self.scw)
            for gi, (t0, n, kind, lo) in enumerate(blk):
                pc, PC = self.proj(csl, CSL, 8, co, self.xn_in, self.XN[gi], gi, lo, n)
                ph, PH = self.proj(hsl, HSL, 8, co, self.xn_in, self.XN[gi], gi, lo, n)
                pb, PBB = self.proj(bsl, BSL, 8, co, self.xn_in, self.XN[gi], gi, lo, n)
                cs, CS = self.r_f512.next()
                self.cp("scalar", cs[:, :n], pc[:, :n], [PC], [CS])
                bs, BS = self.r_f512.next()
                self.cp("scalar", bs[:, :n], pb[:, :n], [PBB], [BS])

                def fill(dst, r, w, cs=cs, ph=ph, n=n, kind=kind, CS=CS, PH=PH):
                    self.tt("vector", dst, cs_view(cs, n, kind), ph_view(ph, n, kind), ALU.mult, r + [CS, PH], w)

                def tail(dst, r, w, cs=cs, ph=ph, n=n, kind=kind, CS=CS, PH=PH):
                    if kind == 'p':
                        self.tt("vector", dst, cs[:, n - 2:n], ph[:, n - 2:n], ALU.mult, r + [CS, PH], w)
                    else:
                        self.tt("vector", dst, cs_view(cs, n, kind)[:, :, LS - 2:LS], ph_view(ph, n, kind)[:, :, LS - 2:LS],
                                ALU.mult, r + [CS, PH], w)
                pcv, PCV = self.conv(c, n, kind, 2, fill, tail, self.carc, self.CARC, self.stc, self.STC, dw, DW)
                self.tt("vector", osc[:, c, lo:lo + n], pcv[:, :n], bs[:, :n], ALU.mult, [PCV, BS], [OSC])
        if ks == "mix_b":
            return
        for part in range(4):
            for hp in range(2):
                sl, SL = self.slab("in", l, 0, 8, [(part * 512 + hp * 256, 256)])
                dws = [self.diagw(l, part * 4 + hp * 2 + h2, 4, self.cqw) for h2 in range(2)] if part < 3 else [None, None]
                for gi, grp in enumerate(blk):
                    gens = [self.qkv_g(l, part, hp * 2 + h2, h2, sl, SL, gi, grp, dws[h2], qkvz, QKVZ) for h2 in range(2)]
                    while gens:
                        for g_ in list(gens):
                            try:
                                next(g_)
                            except StopIteration:
                                gens.remove(g_)
        if ks == "mix_c":
            return
        if last_p:
            self.out_conv_state_prompt(l)
        if has_s:
            self.out_conv_state_sample(l)
        if ks == "mix_d":
            return
        if has_s:
            self.SSAMP, self.WZ, self.KTZ = self.sphase(["ssamp", "wz", "ktz"])
            self.memset("vector", self.wz[:], 0.0, [], [self.WZ])
        ti = 0
        for gi, (t0, n, kind, lo) in enumerate(blk):
            for tt_ in range(n // 128):
                c0 = lo + tt_ * 128
                if kind == 's':
                    for h in range(4):
                        self.gdn_tile(l, h, ti, c0, kind, qkvz, QKVZ, gi)
                else:
                    gens = [self.gdn_gen(l, h, ti, c0, qkvz, QKVZ, gi) for h in range(4)]
                    while gens:
                        for g_ in list(gens):
                            try:
                                next(g_)
                            except StopIteration:
                                gens.remove(g_)
                ti += 1
        if last_p:
            for h in range(4):
                self.dma("sync", self.sem_out, self.o_sp[l, h], self.S[h][:], [self.SB[h]], [])
        for s4 in range(4):
            osl, OSL = self.slab("out", l, 0, 8, [(s4 * 256, 256)])
            for i in range(2):
                dc = s4 * 2 + i
                for gi, (t0, n, kind, lo) in enumerate(blk):
                    ps, PB = self.r_big.next()
                    for kc in range(8):
                        rhs = self.xn[:, kc, lo:lo + n] if kc < 4 else osc[:, kc - 4, lo:lo + n]
                        self.mm(ps[:, :n], osl[:, kc, i * 128:(i + 1) * 128], rhs, kc == 0, kc == 7,
                                [OSL, self.XN[gi], OSC], [PB])
                    X = self.XT[t0]
                    self.tt("vector", self.xT[:, dc, t0:t0 + n], ps[:, :n], self.xT[:, dc, t0:t0 + n], ALU.add, [PB, X], [X])

    def sphase(self, names):
        merged = {}
        for b in self.so_cur:
            if b.w is not None:
                s_, v = b.w
                merged[s_] = max(merged.get(s_, 0), v)
            for s_, v in b.r.items():
                merged[s_] = max(merged.get(s_, 0), v)
        new = []
        for nm in names:
            b = Buf(nm)
            b.r = dict(merged)
            new.append(b)
        self.so_cur = new
        return new

    def diagw(self, l, c, taps, wfn):
        dw, DW = self.r_dw.next()
        for j in range(taps):
            self.ts("vector", dw[:, j, :], self.identb[:], wfn(l, c, j), ALU.mult, [self.CONST], [DW])
        return dw, DW

    def conv(self, c, n, kind, halo, fill, tail, car, CAR, st, ST, dw, DW):
        out = []
        for _ in self.conv_g(out, c, n, kind, halo, fill, tail, car, CAR, st, ST, dw, DW):
            pass
        return out[0]

    def conv_g(self, out, c, n, kind, halo, fill, tail, car, CAR, st, ST, dw, DW):
        taps = halo + 1
        ub, UB = self.r_ub.next()
        pc, PC = self.r_big.next()
        if kind == 'p':
            self.cp("vector", ub[:, 0:halo], car[:, :, c], [CAR[c]], [UB])
            fill(ub[:, halo:halo + n], [], [UB])
            tail(car[:, :, c], [], [CAR[c]])
            yield
            for j in range(taps):
                self.mm(pc[:, :n], dw[:, j, :], ub[:, j:j + n], j == 0, j == taps - 1, [DW, UB], [PC])
        else:
            w_ = LS + halo
            u3 = ub[:, :NSEQ * w_].rearrange("p (s t) -> p s t", t=w_)
            p3 = pc[:, :128].rearrange("p (s t) -> p s t", t=LS)
            self.cp("vector", u3[:, :, 0:halo], st[:, c, :, :], [ST[c]], [UB])
            fill(u3[:, :, halo:halo + LS], [], [UB])
            tail(st[:, c, :, :], [], [ST[c]])
            yield
            for j in range(taps):
                self.mm(p3, dw[:, j, :], u3[:, :, j:j + LS], j == 0, j == taps - 1, [DW, UB], [PC])
        out.append((pc, PC))

    def qkv_g(self, l, part, h, h2, sl, SL, gi, grp, dwp, qkvz, QKVZ):
        C = self.CONST
        (t0, n, kind, lo) = grp
        ps, PS = self.proj(sl, SL, 8, h2 * 128, self.xn_in, self.XN[gi], gi, lo, n)
        dst = qkvz[:, part * 4 + h, lo:lo + n]
        yield
        if part == 3:
            self.act(dst, ps[:, :n], AF.Silu, [PS], [QKVZ])
            return

        def fill(d_, r, w):
            self.cp("scalar", d_, ps_view(ps, n, kind), r + [PS], w)

        def tail(d_, r, w):
            if kind == 'p':
                self.cp("vector", d_, ps[:, n - 3:n], r + [PS], w)
            else:
                self.cp("vector", d_, ps_view(ps, n, kind)[:, :, LS - 3:LS], r + [PS], w)
        out = []
        for _ in self.conv_g(out, part * 4 + h, n, kind, 3, fill, tail, self.carq, self.CARQ, self.stq, self.STQ,
                             dwp[0], dwp[1]):
            yield
        acc, ACC = out[0]
        yield
        if part == 2:
            self.act(dst, acc[:, :n], AF.Silu, [ACC], [QKVZ])
            return
        qs, QS = self.r_f512.next()
        self.act(qs[:, :n], acc[:, :n], AF.Silu, [ACC], [QS])
        yield
        sq, SQ = self.r_b512.next()
        self.act(sq[:, :n], qs[:, :n], AF.Square, [QS], [SQ])
        yield
        p2, P2 = self.r_big.next()
        self.mm(p2[:, :n], self.onesb[:], sq[:, :n], True, True, [SQ, C], [P2])
        yield
        rn, RN = self.r_f512.next()
        self.act(rn[:, :n], p2[:, :n], AF.Ln, [P2, C], [RN], bias=self.cst(0))
        yield
        self.act(rn[:, :n], rn[:, :n], AF.Exp, [RN], [RN], scale=-0.5)
        yield
        self.stt(dst, qs[:, :n], (128.0 ** -0.5) if part == 0 else 1.0, rn[:, :n], ALU.mult, ALU.mult,
                 [QS, RN], [QKVZ])

    def load_sample_conv_state(self, l):
        C = self.CONST
        ident = self.cm(0)
        st, SB_ = self.r_stage.next()
        k = 0
        self.dma("sync", self.stage_sems[k], st[:48, :], self.sqkv[l][:, 0:1024], [], [SB_])
        for grp in range(3):
            if grp == 2:
                st, SB_ = self.r_stage.next()
                k = 0
                self.dma("sync", self.stage_sems[k], st[:48, 0:512], self.sqkv[l][:, 1024:1536], [], [SB_])
                self.dma("sync", self.stage_sems[k], st[:32, 512:1024], self.ssc[l][:, :], [], [SB_])
            ps, PB = self.r_big.next()
            for c4 in range(4):
                off = (c4 if grp == 2 else grp * 4 + c4) * 128
                self.tr(ps[:, c4 * 128:c4 * 128 + 48], st[:48, off:off + 128], ident[:48, :48], [SB_, C], [PB])
            for c4 in range(4):
                c = grp * 4 + c4
                self.cp("vector", self.stq[:, c, :, :], ps[:, c4 * 128:c4 * 128 + 48].rearrange("p (s r) -> p s r", r=3),
                        [PB], [self.STQ[c]])
        ps, PB = self.r_big.next()
        for c in range(4):
            self.tr(ps[:, c * 128:c * 128 + 32], st[:32, 512 + c * 128:512 + (c + 1) * 128], ident[:32, :32], [SB_, C], [PB])
        for c in range(4):
            self.cp("vector", self.stc[:, c, :, :], ps[:, c * 128:c * 128 + 32].rearrange("p (s r) -> p s r", r=2),
                    [PB], [self.STC[c]])

    def out_conv_state_prompt(self, l):
        C = self.CONST
        ident = self.cm(0)
        ps, PB = self.r_big.next()
        self.tr(ps[:36, 0:128], self.carq[:].rearrange("p r c -> p (r c)"), ident, self.CARQ + [C], [PB])
        self.tr(ps[:8, 128:256], self.carc[:].rearrange("p r c -> p (r c)"), ident, self.CARC + [C], [PB])
        st, SB_ = self.r_stage.next()
        k = 0
        self.cp("vector", st[:36, 0:128], ps[:36, 0:128], [PB], [SB_])
        self.cp("vector", st[:8, 128:256], ps[:8, 128:256], [PB], [SB_])
        self.dma("sync", self.stage_sems[k], self.o_qp[l], st[:36, 0:128], [SB_], [])
        self.dma("sync", self.stage_sems[k], self.o_cp[l], st[:8, 128:256], [SB_], [])

    def out_conv_state_sample(self, l):
        C = self.CONST
        ident = self.cm(0)
        for grp in range(3):
            ps, PB = self.r_big.next()
            for c4 in range(4):
                c = grp * 4 + c4
                self.tr(ps[:48, c4 * 128:(c4 + 1) * 128], self.stq[:, c, :, :].rearrange("p s r -> p (s r)"), ident,
                        [self.STQ[c], C], [PB])
            st, SB_ = self.r_stage.next()
            k = 0
            self.cp("vector", st[:48, 0:512], ps[:48, :], [PB], [SB_])
            self.dma("sync", self.stage_sems[k], self.o_qs[l][:, grp * 512:(grp + 1) * 512], st[:48, 0:512], [SB_], [])
        ps, PB = self.r_big.next()
        for c in range(4):
            self.tr(ps[:32, c * 128:(c + 1) * 128], self.stc[:, c, :, :].rearrange("p s r -> p (s r)"), ident,
                    [self.STC[c], C], [PB])
        st, SB_ = self.r_stage.next()
        k = 0
        self.cp("vector", st[:32, 0:512], ps[:32, :], [PB], [SB_])
        self.dma("sync", self.stage_sems[k], self.o_cs[l][:, :], st[:32, 0:512], [SB_], [])

    def gates(self, l, blk, ntile, has_s):
        C = self.CONST
        L = self.L
        gc = self.gcols
        G = self.GCOL
        wsl, WSL = self.slab("in", l, 0, 8, [(OFF_BETA, 8)])
        pba, PBA = self.r_small.next()
        pv = pba[:, :ntile * 8].rearrange("p (t k) -> p t k", k=8)
        ti = 0
        for gi, (t0, n, kind, lo) in enumerate(blk):
            for t_ in range(n // 128):
                c0 = lo + t_ * 128
                for kc in range(8):
                    self.mm(pv[:, ti, :], self.xn[:, kc, c0:c0 + 128], wsl[:, kc, :], kc == 0, kc == 7,
                            [self.XN[gi], WSL], [PBA])
                ti += 1
        tmp, TMP = self.r_f128.next()
        tb = tmp[:, 0:ntile * 4].rearrange("p (t k) -> p t k", k=4)
        ta = tmp[:, 64:64 + ntile * 4].rearrange("p (t k) -> p t k", k=4)
        nt4 = ntile * 4
        dtb = self.crow[:, l * 24:l * 24 + nt4].rearrange("p (t k) -> p t k", k=4)
        nga = self.nega[:, l * 24:l * 24 + nt4].rearrange("p (t k) -> p t k", k=4)
        self.act(tb, pv[:, :, 0:4], AF.Exp, [PBA], [TMP], scale=-1.0)
        self.act(tb, tb, AF.Ln, [TMP, C], [TMP], bias=self.cst(1))
        self.ts("vector", gc[:, 1, :ntile, :], tb, -1.0, ALU.mult, [TMP], [G])
        self.tt("vector", ta, pv[:, :, 4:8], dtb, ALU.add, [PBA, C], [TMP])
        self.act(ta, ta, AF.Exp, [TMP], [TMP])
        self.act(ta, ta, AF.Ln, [TMP, C], [TMP], bias=self.cst(1))
        self.tt("vector", gc[:, 0, :ntile, :], ta, nga, ALU.mult, [TMP, C], [G])
        npt = ntile - 1 if has_s else ntile
        pc, PC = self.r_small.next()
        pl, PL = self.r_small.next()
        pcv = pc[:, :nt4].rearrange("p (t k) -> p t k", k=4)
        plv = pl[:, :nt4].rearrange("p (t k) -> p t k", k=4)
        if npt > 0:
            self.mm(pcv[:, :npt, :], self.cm(1), gc[:, 0, :npt, :], True, True, [G, C], [PC])
            self.mm(plv[:, :npt, :], self.cm(2), gc[:, 0, :npt, :], True, True, [G, C], [PL])
        if has_s:
            self.mm(pcv[:, npt:ntile, :], self.cm(3), gc[:, 0, npt:ntile, :], True, True, [G, C], [PC])
            self.mm(plv[:, npt:ntile, :], self.cm(4), gc[:, 0, npt:ntile, :], True, True, [G, C], [PL])
        self.cp("vector", gc[:, 2, :ntile, :], pcv, [PC], [G])
        t2, T2 = self.r_f128.next()
        t2a = t2[:, 0:nt4].rearrange("p (t k) -> p t k", k=4)
        t2b = t2[:, 64:64 + nt4].rearrange("p (t k) -> p t k", k=4)
        self.tt("vector", t2a, pcv, gc[:, 1, :ntile, :], ALU.add, [PC, G], [T2])
        self.act(gc[:, 3, :ntile, :], t2a, AF.Exp, [T2], [G])
        self.tt("vector", t2b, plv, gc[:, 2, :ntile, :], ALU.subtract, [PL, G], [T2])
        self.act(gc[:, 4, :ntile, :], t2b, AF.Exp, [T2], [G])
        self.act(gc[:, 5, :ntile, :], gc[:, 1, :ntile, :], AF.Exp, [G], [G])

    def gdn_gen(self, l, h, ti, c0, qkvz, QKVZ, gi):
        C = self.CONST
        G = self.GCOL
        gc = self.gcols
        mi, ms, tri, identf = self.cm(5), self.cm(6), self.cm(1), self.cm(0)
        qT = qkvz[:, h, c0:c0 + 128]
        kT = qkvz[:, 4 + h, c0:c0 + 128]
        vT = qkvz[:, 8 + h, c0:c0 + 128]
        zs = qkvz[:, 12 + h, c0:c0 + 128]
        col = lambda k: gc[:, k, ti, h:h + 1]
        bc = lambda k: gc[:, k, ti, h:h + 1].broadcast_to([128, 128])
        S_, Sb_, SBF = self.S[h], self.Sb[h], self.SB[h]
        ps = self.hps[h]
        hb = self.hb[h]
        eg, EG = hb["eg"]
        oi, OI = hb["oi"]
        qkd, QKD = hb["qkd"]
        qdec, QDEC = hb["qdec"]
        kbg, KBG = hb["kbg"]
        ktl, KTL = hb["ktl"]
        vb, VB = hb["vb"]
        nw, NW = hb["nw"]
        vn, VN = hb["vn"]
        pg1, PB_ = ps.next()
        self.mm(pg1, bc(0), tri, True, True, [G, C], [PB_])
        pg2, _ = ps.next()
        self.mm(pg2, bc(0), tri, True, False, [G, C], [PB_])
        self.mm(pg2, bc(1), identf, False, True, [G, C], [PB_])
        pG, _ = ps.next()
        self.mm(pG, kT, kT, True, True, [QKVZ], [PB_])
        pQ, _ = ps.next()
        self.mm(pQ, kT, qT, True, True, [QKVZ], [PB_])
        yield
        d1, D1 = oi, OI
        self.stt(d1[:], pg1, col(2), mi, ALU.subtract, ALU.add, [PB_, G, C], [D1])
        d2, D2 = self.r_f128.next()
        self.stt(d2[:], pg2, col(2), ms, ALU.subtract, ALU.add, [PB_, G, C], [D2])
        self.act(eg[:], pg1, AF.Exp, [PB_], [EG])
        self.act(d1[:], d1[:], AF.Exp, [D1], [D1])
        self.act(d2[:], d2[:], AF.Exp, [D2], [D2])
        yield
        Bm, BM = self.r_b128.next()
        self.tt("vector", Bm[:], pG, d2[:], ALU.mult, [PB_, D2], [BM])
        self.tt("vector", qkd[:], pQ, d1[:], ALU.mult, [PB_, D1], [QKD])
        self.tt("vector", qdec[:], qT, eg[:], ALU.mult, [QKVZ, EG], [QDEC])
        yield
        pk_, _ = ps.next()
        pkt = pk_.bitcast(BF16)[:, 0:128]
        self.tr(pkt, kT, self.identb[:], [QKVZ, C], [PB_])
        pv_, _ = ps.next()
        pvt = pv_.bitcast(BF16)[:, 0:128]
        self.tr(pvt, vT, self.identb[:], [QKVZ, C], [PB_])
        pa_, _ = ps.next()
        pat = pa_.bitcast(BF16)[:, 0:128]
        self.tr(pat, Bm[:], self.identb[:], [BM, C], [PB_])
        yield
        self.act(kbg[:], pkt, AF.Copy, [PB_, G], [KBG], scale=col(3))
        self.ts("vector", ktl[:], pkt, col(4), ALU.mult, [PB_, G], [KTL])
        self.act(vb[:], pvt, AF.Copy, [PB_, G], [VB], scale=col(5))
        Am, AM = self.r_b128.next()
        self.cp("scalar", Am[:], pat, [PB_], [AM])
        Pm, PM = self.r_b128.next()
        self.tt("vector", Pm[:], self.identb[:], Bm[:], ALU.subtract, [C, BM], [PM])
        yield
        nlev = 5
        Ak, AK, Bk, BK = Am, AM, Bm, BM
        for lev in range(nlev):
            pa2, _ = ps.next()
            self.mm(pa2, Bk[:], Ak[:], True, True, [BK, AK], [PB_])
            if lev < nlev - 1:
                pb2, _ = ps.next()
                self.mm(pb2, Ak[:], Bk[:], True, True, [BK, AK], [PB_])
            yield
            An, AN = self.r_b128.next()
            self.cp("scalar", An[:], pa2, [PB_], [AN])
            if lev < nlev - 1:
                Bn, BN = self.r_b128.next()
                self.cp("vector", Bn[:], pb2, [PB_], [BN])
            yield
            pp, _ = ps.next()
            self.mm(pp, An[:], Pm[:], True, True, [AN, PM], [PB_])
            yield
            Pn, PN = self.r_b128.next()
            self.tt("vector", Pn[:], pp, Pm[:], ALU.add, [PB_, PM], [PN])
            Pm, PM = Pn, PN
            Ak, AK = An, AN
            if lev < nlev - 1:
                Bk, BK = Bn, BN
            yield
        TT, TTB = Pm, PM
        pw, _ = ps.next()
        self.mm(pw, kbg[:], TT[:], True, True, [KBG, TTB], [PB_])
        yield
        self.act(nw[:], pw, AF.Copy, [PB_], [NW], scale=-1.0)
        yield
        for c in range(2):
            sl = slice(64 * c, 64 * c + 64)
            po_i, _ = ps.next()
            self.mm(po_i[:, sl], Sb_[:], qdec[:, sl], True, True, [SBF, QDEC], [PB_])
            pvn, _ = ps.next()
            self.mm(pvn[sl, :], TT[:, sl], vb[:], True, False, [TTB, VB], [PB_])
            self.mm(pvn[sl, :], nw[:, sl], Sb_[:], False, True, [NW, SBF], [PB_])
            yield
            self.cp("scalar", vn[sl, :], pvn[sl, :], [PB_], [VN])
            self.cp("vector", oi[:, sl], po_i[:, sl], [PB_], [OI])
            yield
            pS, _ = ps.next()
            self.mm(pS, ktl[sl, :], vn[sl, :], True, True, [KTL, VN], [PB_])
            yield
            self.stt(S_[:], S_[:], eg[:, 64 * c + 63:64 * c + 64], pS, ALU.mult, ALU.add, [SBF, EG, PB_], [SBF])
            yield
            self.cp("scalar", Sb_[:], S_[:], [SBF], [SBF])
            yield
        po, _ = ps.next()
        self.mm(po, vn[:], qkd[:], True, True, [VN, QKD], [PB_])
        yield
        self.tt("vector", oi[:], po, oi[:], ALU.add, [PB_, OI], [OI])
        yield
        sq, SQ = self.r_b128.next()
        self.act(sq[:], oi[:], AF.Square, [OI], [SQ])
        yield
        pss, _ = ps.next()
        self.mm(pss, self.onesb[:], sq[:], True, True, [SQ, C], [PB_])
        yield
        rn, RN = self.r_f128.next()
        self.act(rn[:], pss, AF.Ln, [PB_, C], [RN], scale=1.0 / 128, bias=self.cst(0))
        yield
        self.act(rn[:], rn[:], AF.Exp, [RN], [RN], scale=-0.5)
        self.stt(oi[:], oi[:], self.ggdn(l), rn[:], ALU.mult, ALU.mult, [OI, RN, C], [OI])
        self.tt("vector", self.xn[:, h, c0:c0 + 128], oi[:], zs, ALU.mult, [OI, QKVZ], [self.XN[gi]])

    def gdn_tile(self, l, h, ti, c0, kind, qkvz, QKVZ, gi):
        C = self.CONST
        G = self.GCOL
        gc = self.gcols
        samp = (kind == 's')
        mi, ms = (self.cm(7), self.cm(8)) if samp else (self.cm(5), self.cm(6))
        tri = self.cm(3) if samp else self.cm(1)
        identf = self.cm(0)
        qT = qkvz[:, h, c0:c0 + 128]
        kT = qkvz[:, 4 + h, c0:c0 + 128]
        vT = qkvz[:, 8 + h, c0:c0 + 128]
        zs = qkvz[:, 12 + h, c0:c0 + 128]
        col = lambda k: gc[:, k, ti, h:h + 1]
        bc = lambda k: gc[:, k, ti, h:h + 1].broadcast_to([128, 128])
        S_, Sb_, SBF = self.S[h], self.Sb[h], self.SB[h]
        pg1, PG1 = self.r_small.next()
        self.mm(pg1, bc(0), tri, True, True, [G, C], [PG1])
        pg2, PG2 = self.r_small.next()
        self.mm(pg2, bc(0), tri, True, False, [G, C], [PG2])
        self.mm(pg2, bc(1), identf, False, True, [G, C], [PG2])
        pG, PGG = self.r_small.next()
        self.mm(pG, kT, kT, True, True, [QKVZ], [PGG])
        pQ, PQQ = self.r_small.next()
        self.mm(pQ, kT, qT, True, True, [QKVZ], [PQQ])
        pk_, PKT = self.r_small.next()
        pkt = pk_.bitcast(BF16)[:, 0:128]
        self.tr(pkt, kT, self.identb[:], [QKVZ, C], [PKT])
        pv_, PVT = self.r_small.next()
        pvt = pv_.bitcast(BF16)[:, 0:128]
        self.tr(pvt, vT, self.identb[:], [QKVZ, C], [PVT])
        d1, D1 = self.r_f128.next()
        self.stt(d1[:], pg1, col(2), mi, ALU.subtract, ALU.add, [PG1, G, C], [D1])
        self.act(d1[:], d1[:], AF.Exp, [D1], [D1])
        d2, D2 = self.r_f128.next()
        self.stt(d2[:], pg2, col(2), ms, ALU.subtract, ALU.add, [PG2, G, C], [D2])
        self.act(d2[:], d2[:], AF.Exp, [D2], [D2])
        eg, EG = self.r_f128.next()
        self.act(eg[:], pg1, AF.Exp, [PG1], [EG])
        Bm, BM = self.r_b128.next()
        self.tt("vector", Bm[:], pG, d2[:], ALU.mult, [PGG, D2], [BM])
        qkd, QKD = self.r_b128.next()
        self.tt("vector", qkd[:], pQ, d1[:], ALU.mult, [PQQ, D1], [QKD])
        qdec, QDEC = self.r_b128.next()
        self.tt("vector", qdec[:], qT, eg[:], ALU.mult, [QKVZ, EG], [QDEC])
        kbg, KBG = self.r_b128.next()
        self.act(kbg[:], pkt, AF.Copy, [PKT, G], [KBG], scale=col(3))
        ktl, KTL = self.r_b128.next()
        self.ts("vector", ktl[:], pkt, col(4), ALU.mult, [PKT, G], [KTL])
        vb, VB = self.r_b128.next()
        self.act(vb[:], pvt, AF.Copy, [PVT, G], [VB], scale=col(5))
        import os
        ks = os.environ.get("KSTOP", "full")
        if ks.endswith("g1"):
            return
        pa_, PA = self.r_small.next()
        pat = pa_.bitcast(BF16)[:, 0:128]
        self.tr(pat, Bm[:], self.identb[:], [BM, C], [PA])
        Am, AM = self.r_b128.next()
        self.cp("scalar", Am[:], pat, [PA], [AM])
        Pm, PM = self.r_b128.next()
        self.tt("vector", Pm[:], self.identb[:], Bm[:], ALU.subtract, [C, BM], [PM])
        nlev = 2 if samp else 5
        Ak, AK, Bk, BK = Am, AM, Bm, BM
        for lev in range(nlev):
            pa2, PA2 = self.r_small.next()
            self.mm(pa2, Bk[:], Ak[:], True, True, [BK, AK], [PA2])
            An, AN = self.r_b128.next()
            self.cp("scalar", An[:], pa2, [PA2], [AN])
            if lev < nlev - 1:
                pb2, PB2 = self.r_small.next()
                self.mm(pb2, Ak[:], Bk[:], True, True, [BK, AK], [PB2])
                Bn, BN = self.r_b128.next()
                self.cp("vector", Bn[:], pb2, [PB2], [BN])
            pp, PP = self.r_small.next()
            self.mm(pp, An[:], Pm[:], True, True, [AN, PM], [PP])
            Pn, PN = self.r_b128.next()
            self.tt("vector", Pn[:], pp, Pm[:], ALU.add, [PP, PM], [PN])
            Pm, PM = Pn, PN
            Ak, AK = An, AN
            if lev < nlev - 1:
                Bk, BK = Bn, BN
        TT, TTB = Pm, PM
        if ks.endswith("g2"):
            return
        pw, PW = self.r_small.next()
        self.mm(pw, kbg[:], TT[:], True, True, [KBG, TTB], [PW])
        vn, VN = self.r_b128.next()
        po_i, POI = self.r_small.next()
        if not samp:
            nw, NW = self.r_b128.next()
            self.act(nw[:], pw, AF.Copy, [PW], [NW], scale=-1.0)
            for c in range(2):
                sl = slice(64 * c, 64 * c + 64)
                self.mm(po_i[:, sl], Sb_[:], qdec[:, sl], True, True, [SBF, QDEC], [POI])
                pvn, PVN = self.r_small.next()
                self.mm(pvn[sl, :], TT[:, sl], vb[:], True, False, [TTB, VB], [PVN])
                self.mm(pvn[sl, :], nw[:, sl], Sb_[:], False, True, [NW, SBF], [PVN])
                self.cp("scalar", vn[sl, :], pvn[sl, :], [PVN], [VN])
                pS, PSS = self.r_small.next()
                self.mm(pS, ktl[sl, :], vn[sl, :], True, True, [KTL, VN], [PSS])
                self.stt(S_[:], S_[:], eg[:, 64 * c + 63:64 * c + 64], pS, ALU.mult, ALU.add, [SBF, EG, PSS], [SBF])
                self.cp("scalar", Sb_[:], S_[:], [SBF], [SBF])
        else:
            SS = self.SSAMP
            self.dma("sync", self.sem_ss, self.ssamp[:], self.sgdn[l, :, h].rearrange("s k v -> k s v"), [], [SS])
            self.cp("vector", self.ssampb[:], self.ssamp[:], [SS], [SS])
            wzd = self.wz[:].rearrange("p s i -> p (s i)")
            dst = bass.AP(tensor=wzd.tensor, offset=wzd.offset, ap=[list(wzd.ap[0]), [136, NSEQ], [1, LS]])
            self.act(dst, pw.rearrange("p (s j) -> p s j", j=LS), AF.Copy, [PW], [self.WZ], scale=-1.0)
            bm16 = self.cmat[:, 9 * 128:9 * 128 + 16]
            self.tt("vector", self.ktz[:], ktl[:].unsqueeze(1).broadcast_to([128, NSEQ, 128]),
                    bm16.unsqueeze(2).broadcast_to([128, NSEQ, 128]), ALU.mult, [KTL, C], [self.KTZ])
            for s in range(NSEQ):
                self.mm(po_i[:, s * LS:(s + 1) * LS], self.ssampb[:, s, :], qdec[:, s * LS:(s + 1) * LS], True, True,
                        [SS, QDEC], [POI])
            oi, OI = self.r_f128.next()
            self.cp("scalar", oi[:], po_i, [POI], [OI])
            pvn, PVN = self.r_small.next()
            self.mm(pvn, TT[:], vb[:], True, False, [TTB, VB], [PVN])
            for s in range(NSEQ):
                self.mm(pvn, self.wz[:, s, :], self.ssampb[:, s, :], False, s == NSEQ - 1, [self.WZ, SS], [PVN])
            self.cp("scalar", vn[:], pvn, [PVN], [VN])
            for s in range(NSEQ):
                pS, PSS = self.r_small.next()
                self.mm(pS, self.ktz[:, s, :], vn[:], True, True, [self.KTZ, VN], [PSS])
                self.stt(self.ssamp[:, s, :], self.ssamp[:, s, :], eg[:, s * LS + LS - 1:s * LS + LS], pS,
                         ALU.mult, ALU.add, [SS, EG, PSS], [SS])
            self.dma("sync", self.sem_ss, self.o_ss[l, :, h].rearrange("s k v -> k s v"), self.ssamp[:], [SS], [])
        if ks.endswith("g3"):
            return
        po, PO = self.r_small.next()
        self.mm(po, vn[:], qkd[:], True, True, [VN, QKD], [PO])
        if not samp:
            oi, OI = self.r_f128.next()
            self.cp("scalar", oi[:], po_i, [POI], [OI])
        self.tt("vector", oi[:], po, oi[:], ALU.add, [PO, OI], [OI])
        sq, SQ = self.r_b128.next()
        self.act(sq[:], oi[:], AF.Square, [OI], [SQ])
        pss, PSS2 = self.r_small.next()
        self.mm(pss, self.onesb[:], sq[:], True, True, [SQ, C], [PSS2])
        rn, RN = self.r_f128.next()
        self.act(rn[:], pss, AF.Ln, [PSS2, C], [RN], scale=1.0 / 128, bias=self.cst(0))
        self.act(rn[:], rn[:], AF.Exp, [RN], [RN], scale=-0.5)
        self.stt(oi[:], oi[:], self.ggdn(l), rn[:], ALU.mult, ALU.mult, [OI, RN, C], [OI])
        self.tt("vector", self.xn[:, h, c0:c0 + 128], oi[:], zs, ALU.mult, [OI, QKVZ], [self.XN[gi]])

    def attn(self, l, bi, blk):
        TBM = self.TBM
        C = self.CONST
        self.norm(l, 2, blk)
        ov = self.ovl
        qT = ov[:, 0:8 * TBM].rearrange("p (c n) -> p c n", n=TBM)
        aoT = ov[:, 8 * TBM:16 * TBM].rearrange("p (c n) -> p c n", n=TBM)
        QT, AO, self.KV = self.phase(["qT", "aoT", "kv"])
        self.mem_kv(l, bi == 0)
        for s4 in range(4):
            sl, SL = self.slab("xq", l, 0, 8, [(s4 * 256, 256)])
            for i in range(2):
                for gi, (t0, n, kind, lo) in enumerate(blk):
                    ps, PB = self.proj(sl, SL, 8, i * 128, self.xn_in, self.XN[gi], gi, lo, n)
                    self.cp("scalar", qT[:, s4 * 2 + i, lo:lo + n], ps[:, :n], [PB], [QT])
        for gi, (t0, n, kind, lo) in enumerate(blk):
            if kind == 'p':
                for hp in range(2):
                    self.rr([self.attn_p_g(self.KT, self.Vb, [self.KV], qT, QT, aoT, AO, hp * 2 + i, lo, n) for i in range(2)])
            else:
                KVSB = self.sphase(["kvs0", "kvs1", "kts"])
                kts, KTS = self.kts_t, KVSB[2]
                for s in range(NSEQ):
                    k_ = s % 2
                    kv, KVS = self.kvs_t[k_], KVSB[k_]
                    self.dma("gpsimd", self.kvs_sems[k_], kv[:, 0], self.cK[l, s].rearrange("(t p) d -> p t d", p=128), [], [KVS])
                    self.dma("gpsimd", self.kvs_sems[k_], kv[:, 1], self.cV[l, s].rearrange("(t p) d -> p t d", p=128), [], [KVS])
                    for mt in range(2):
                        for half in range(2):
                            pb_, PB = self.r_big4.next()
                            pbt = pb_.bitcast(BF16)[:, 0:512]
                            for c4 in range(4):
                                c = half * 4 + c4
                                self.tr(pbt[:, c4 * 128:(c4 + 1) * 128], kv[:, 0, mt, c * 128:(c + 1) * 128], self.identb[:],
                                        [KVS, C], [PB])
                            self.cp("vector" if half else "scalar", kts[:, half * 4:half * 4 + 4, mt * 128:(mt + 1) * 128],
                                    pbt.rearrange("p (c n) -> p c n", n=128), [PB], [KTS])
                    gens = [self.attn_core_g(kts, kv[:, 1], [KTS, KVS], qT, QT, aoT, AO, h, lo + s * LS, LS) for h in range(4)]
                    while gens:
                        for g_ in list(gens):
                            try:
                                next(g_)
                            except StopIteration:
                                gens.remove(g_)
        for s4 in range(4):
            sl, SL = self.slab("xo", l, 0, 8, [(s4 * 256, 256)])
            for i in range(2):
                dc = s4 * 2 + i
                for gi, (t0, n, kind, lo) in enumerate(blk):
                    ps, PB = self.r_big.next()
                    for kc in range(8):
                        self.mm(ps[:, :n], sl[:, kc, i * 128:(i + 1) * 128], aoT[:, kc, lo:lo + n], kc == 0, kc == 7,
                                [SL, AO], [PB])
                    X = self.XT[t0]
                    self.tt("vector", self.xT[:, dc, t0:t0 + n], ps[:, :n], self.xT[:, dc, t0:t0 + n], ALU.add, [PB, X], [X])

    def attn_p_g(self, KT, V, rd, qT, QT, aoT, AO, h, lo, n):
        C = self.CONST
        pss, ex, pos = [], [], []
        for mt in range(2):
            ps, PB = self.r_big.next()
            for dc in range(2):
                self.mm(ps[:, :n], KT[:, 2 * h + dc, mt * 128:(mt + 1) * 128], qT[:, 2 * h + dc, lo:lo + n], dc == 0, dc == 1,
                        rd + [QT], [PB])
            pss.append((ps, PB))
        yield
        for mt in range(2):
            e, E = self.r_b512.next()
            self.act(e[:, :n], pss[mt][0][:, :n], AF.Exp, [pss[mt][1]], [E], scale=1.0 / 16.0)
            ex.append((e, E))
        yield
        pd, PD = self.r_big.next()
        for mt in range(2):
            self.mm(pd[:, :n], self.onesb[:], ex[mt][0][:, :n], mt == 0, mt == 1, [ex[mt][1], C], [PD])
        for dc in range(2):
            po, PO = self.r_big.next()
            for mt in range(2):
                self.mm(po[:, :n], V[:, mt, (2 * h + dc) * 128:(2 * h + dc + 1) * 128], ex[mt][0][:, :n], mt == 0, mt == 1,
                        rd + [ex[mt][1]], [PO])
            pos.append((po, PO))
        yield
        rd_, RD = self.r_f512.next()
        self.act(rd_[:, :n], pd[:, :n], AF.Ln, [PD], [RD])
        self.act(rd_[:, :n], rd_[:, :n], AF.Exp, [RD], [RD], scale=-1.0)
        yield
        for dc in range(2):
            self.tt("vector", aoT[:, 2 * h + dc, lo:lo + n], pos[dc][0][:, :n], rd_[:, :n], ALU.mult, [pos[dc][1], RD], [AO])

    def attn_core_g(self, KT, V, rd, qT, QT, aoT, AO, h, lo, n):
        C = self.CONST
        pss, ex, pos = [], [], []
        for mt in range(2):
            ps, PB = self.r_small.next()
            for dc in range(2):
                self.mm(ps[:, :n], KT[:, 2 * h + dc, mt * 128:(mt + 1) * 128], qT[:, 2 * h + dc, lo:lo + n], dc == 0, dc == 1,
                        rd + [QT], [PB])
            pss.append((ps, PB))
        yield
        for mt in range(2):
            e, E = self.r_b128.next()
            self.act(e[:, :n], pss[mt][0][:, :n], AF.Exp, [pss[mt][1]], [E], scale=1.0 / 16.0)
            ex.append((e, E))
        yield
        pd, PD = self.r_small.next()
        for mt in range(2):
            self.mm(pd[:, :n], self.onesb[:], ex[mt][0][:, :n], mt == 0, mt == 1, [ex[mt][1], C], [PD])
        for dc in range(2):
            po, PO = self.r_small.next()
            for mt in range(2):
                self.mm(po[:, :n], V[:, mt, (2 * h + dc) * 128:(2 * h + dc + 1) * 128], ex[mt][0][:, :n], mt == 0, mt == 1,
                        rd + [ex[mt][1]], [PO])
            pos.append((po, PO))
        yield
        rd_, RD = self.r_f128.next()
        self.act(rd_[:, :n], pd[:, :n], AF.Ln, [PD], [RD])
        self.act(rd_[:, :n], rd_[:, :n], AF.Exp, [RD], [RD], scale=-1.0)
        yield
        for dc in range(2):
            self.tt("vector", aoT[:, 2 * h + dc, lo:lo + n], pos[dc][0][:, :n], rd_[:, :n], ALU.mult, [pos[dc][1], RD], [AO])

    def attn_core(self, KT, V, KVB, qT, QT, aoT, AO, h, lo, n, KVB2=None):
        C = self.CONST
        rd = [KVB] + ([KVB2] if KVB2 is not None else [])
        ex = []
        for mt in range(2):
            ps, PB = self.r_big.next()
            for dc in range(2):
                self.mm(ps[:, :n], KT[:, 2 * h + dc, mt * 128:(mt + 1) * 128], qT[:, 2 * h + dc, lo:lo + n], dc == 0, dc == 1,
                        rd + [QT], [PB])
            e, E = self.r_b512.next()
            self.act(e[:, :n], ps[:, :n], AF.Exp, [PB], [E], scale=1.0 / 16.0)
            ex.append((e, E))
        pd, PD = self.r_big.next()
        for mt in range(2):
            self.mm(pd[:, :n], self.onesb[:], ex[mt][0][:, :n], mt == 0, mt == 1, [ex[mt][1], C], [PD])
        rd_, RD = self.r_f512.next()
        self.act(rd_[:, :n], pd[:, :n], AF.Ln, [PD], [RD])
        self.act(rd_[:, :n], rd_[:, :n], AF.Exp, [RD], [RD], scale=-1.0)
        for dc in range(2):
            po, PO = self.r_big.next()
            for mt in range(2):
                self.mm(po[:, :n], V[:, mt, (2 * h + dc) * 128:(2 * h + dc + 1) * 128], ex[mt][0][:, :n], mt == 0, mt == 1,
                        rd + [ex[mt][1]], [PO])
            self.tt("vector", aoT[:, 2 * h + dc, lo:lo + n], po[:, :n], rd_[:, :n], ALU.mult, [PO, RD], [AO])

    def mem_kv(self, l, emit_out):
        C = self.CONST
        memin = lambda kc, lo, n: self.memT[:, kc, lo:lo + n]
        for which in range(2):
            wn = "xk" if which == 0 else "xv"
            outd = self.o_mk if which == 0 else self.o_mv
            for s4 in range(4):
                sl, SL = self.slab(wn, l, 0, 8, [(s4 * 256, 256)])
                if which == 0:
                    for i in range(2):
                        ps, PB = self.proj(sl, SL, 8, i * 128, memin, self.MEMT, 0, 0, MEM)
                        self.cp("scalar", self.KT[:, s4 * 2 + i, :], ps[:, :MEM], [PB], [self.KV])
                for mt in range(2):
                    ps, PB = self.r_big.next()
                    for kc in range(8):
                        self.mm(ps[:, :256], self.memT[:, kc, mt * 128:(mt + 1) * 128], sl[:, kc, :], kc == 0, kc == 7,
                                [self.MEMT, SL], [PB])
                    if which == 1:
                        self.cp("scalar", self.Vb[:, mt, s4 * 256:(s4 + 1) * 256], ps[:, :256], [PB], [self.KV])
                    if emit_out:
                        st, SB_ = self.r_stage.next()
                        k = 0
                        self.cp("vector", st[:, 0:256], ps[:, :256], [PB], [SB_])
                        self.dma("sync", self.stage_sems[k], outd[l, mt * 128:(mt + 1) * 128, s4 * 256:(s4 + 1) * 256],
                                 st[:, 0:256], [SB_], [])

    def final(self):
        C = self.CONST
        identf = self.cm(0)
        for blk in self.blocks:
            for gi, (t0, n, kind, lo) in enumerate(blk):
                X = self.XT[t0]
                rs, RS = self.sumsq(X, t0, n)
                for c in range(8):
                    self.stt(self.xT[:, c, t0:t0 + n], self.xT[:, c, t0:t0 + n], self.gfin(c), rs[:, :n],
                             ALU.mult, ALU.mult, [X, RS, C], [X])
                for t_ in range(n // 128):
                    tok = t0 + t_ * 128
                    st, SB_ = self.r_stage.next()
                    k = 0
                    for half in range(2):
                        pb_, PB2 = self.r_big.next()
                        for c4 in range(4):
                            c = half * 4 + c4
                            self.tr(pb_[:, c4 * 128:(c4 + 1) * 128], self.xT[:, c, tok:tok + 128], identf, [X, C], [PB2])
                        self.cp("vector" if half else "scalar", st[:, half * 512:(half + 1) * 512], pb_[:], [PB2], [SB_])
                    dst = self.yp[tok:tok + 128, :] if kind == 'p' else self.ys[:, :]
                    self.dma("sync", self.stage_sems[k], dst, st[:], [SB_], [])


def cs_view(cs, n, kind):
    return cs[:, :n] if kind == 'p' else cs[:, :n].rearrange("p (s t) -> p s t", t=LS)


def ph_view(ph, n, kind):
    return ph[:, :n] if kind == 'p' else ph[:, :n].rearrange("p (s t) -> p s t", t=LS)


ps_view = ph_view


def host_consts(L, g_ffn1, g_mix, g_xattn, g_ffn2, g_final, conv_qkv_w, sconv_w, g_gdn_out, a_log, dt_bias):
    f = np.float32
    fm = lambda v: np.ascontiguousarray(np.asarray(v, f).reshape(-1, 128).T)
    NV = L * 32 + 8 + L * 48 + L * 12 + L + 4
    cvec = np.zeros((128, NV), f)
    for l in range(L):
        for wi, g in enumerate((g_ffn1, g_mix, g_xattn, g_ffn2)):
            cvec[:, (l * 4 + wi) * 8:(l * 4 + wi) * 8 + 8] = fm(g[l])
    o = L * 32
    cvec[:, o:o + 8] = fm(g_final)
    o += 8
    for l in range(L):
        for c in range(12):
            cvec[:, o + (l * 12 + c) * 4:o + (l * 12 + c) * 4 + 4] = np.asarray(conv_qkv_w[l], f)[:, c * 128:(c + 1) * 128].T
    o += L * 48
    for l in range(L):
        for c in range(4):
            cvec[:, o + (l * 4 + c) * 3:o + (l * 4 + c) * 3 + 3] = np.asarray(sconv_w[l], f)[:, c * 128:(c + 1) * 128].T
    o += L * 12
    for l in range(L):
        cvec[:, o + l] = np.asarray(g_gdn_out[l], f)
    o += L
    cvec[:, o] = 1e-6
    cvec[:, o + 1] = 1.0
    crow = np.zeros((128, 2 * L * 24), f)
    for l in range(L):
        crow[:, l * 24:(l + 1) * 24] = np.tile(np.asarray(dt_bias[l], f), 6)[None, :]
        crow[:, L * 24 + l * 24:L * 24 + (l + 1) * 24] = np.tile(np.asarray(a_log[l], f), 6)[None, :]
    cmat = np.zeros((128, 9 * 128 + 16), f)
    j = np.arange(128)[:, None]
    i = np.arange(128)[None, :]
    cmat[:, 0:128] = (i == j)
    for k, bs in ((0, 64), (1, 8)):
        same = (i // bs) == (j // bs)
        cmat[:, (1 + 2 * k) * 128:(2 + 2 * k) * 128] = same & (j <= i)
        cmat[:, (2 + 2 * k) * 128:(3 + 2 * k) * 128] = same
        cmat[:, (5 + 2 * k) * 128:(6 + 2 * k) * 128] = np.where(same & (i >= j), 0.0, NEG)
        cmat[:, (6 + 2 * k) * 128:(7 + 2 * k) * 128] = np.where(same & (i > j), 0.0, NEG)
    cmat[:, 9 * 128:9 * 128 + 16] = (j // 8) == np.arange(16)[None, :]
    return cvec, crow, cmat


_CACHE = {}


def get_program(L, SEQ):
    key = (L, SEQ)
    if key not in _CACHE:
        d = Builder(bass.Bass("TRN2", target_bir_lowering=False), L, SEQ, dry=True)
        d.build()
        nc = bass.Bass("TRN2", target_bir_lowering=False)
        b = Builder(nc, L, SEQ, dry=False, log=d.slab_log)
        b.build()
        _CACHE[key] = nc
    return _CACHE[key]


def kernel(x_prompt, x_sample, mem_prompt, state_gdn, state_qkv_conv, state_short_conv,
           cache_mem_k, cache_mem_v, g_ffn1, w_ffn1_gu, w_ffn1_down, g_mix, w_in, conv_qkv_w,
           a_log, dt_bias, g_gdn_out, sconv_w, w_out, g_xattn, w_xq, w_xk, w_xv, w_xo,
           g_ffn2, w_ffn2_gu, w_ffn2_down, g_final, _ncores=8, _runner=None):
    f = np.float32
    A = lambda v: np.ascontiguousarray(np.asarray(v, dtype=f))
    L = int(np.asarray(w_in).shape[0])
    SEQ = int(np.asarray(x_prompt).shape[1])
    NC_ = _ncores
    nc = get_program(L, SEQ)
    cvec, crow, cmat = host_consts(L, g_ffn1, g_mix, g_xattn, g_ffn2, g_final, conv_qkv_w, sconv_w,
                                   g_gdn_out, a_log, dt_bias)
    xpr, xsa, mem = A(x_prompt), A(x_sample), A(mem_prompt)
    sg, sq, sc = A(state_gdn), A(state_qkv_conv), A(state_short_conv)
    ck, cv = A(cache_mem_k), A(cache_mem_v)
    shared = {"w_ffn1_gu": A(w_ffn1_gu), "w_ffn1_down": A(w_ffn1_down), "w_in": A(w_in), "w_out": A(w_out),
              "w_xq": A(w_xq), "w_xk": A(w_xk), "w_xv": A(w_xv), "w_xo": A(w_xo),
              "w_ffn2_gu": A(w_ffn2_gu), "w_ffn2_down": A(w_ffn2_down), "cvec": cvec, "crow": crow, "cmat": cmat}
    in_maps = []
    for c in range(NC_):
        b0, b1 = c * NSEQ, (c + 1) * NSEQ
        m = dict(shared)
        m["xp"] = xpr[c]
        m["xs"] = np.ascontiguousarray(xsa[b0:b1].reshape(NSEQ * LS, D))
        m["mem"] = mem[c]
        m["sgdn"] = np.ascontiguousarray(sg[:, b0:b1])
        m["sqkv"] = np.ascontiguousarray(sq[:, b0:b1].reshape(L, NSEQ * 3, 1536))
        m["ssc"] = np.ascontiguousarray(sc[:, b0:b1].reshape(L, NSEQ * 2, 512))
        m["cK"] = np.ascontiguousarray(ck[:, b0:b1].reshape(L, NSEQ, MEM, D))
        m["cV"] = np.ascontiguousarray(cv[:, b0:b1].reshape(L, NSEQ, MEM, D))
        in_maps.append(m)
    if _runner is None:
        res = run_bass_kernel_spmd(nc, in_maps, core_ids=list(range(NC_))).results
    else:
        res = _runner(nc, in_maps)
    st = lambda k: np.stack([np.asarray(r[k], f) for r in res], axis=0)
    yp = st("yp")
    ys = st("ys").reshape(NC_ * NSEQ, LS, D)
    p_s = np.ascontiguousarray(st("o_sp").transpose(1, 0, 2, 3, 4))
    p_q = np.ascontiguousarray(st("o_qp").reshape(NC_, L, 3, 1536).transpose(1, 0, 2, 3))
    p_c = np.ascontiguousarray(st("o_cp").reshape(NC_, L, 2, 512).transpose(1, 0, 2, 3))
    p_mk = np.ascontiguousarray(st("o_mk").reshape(NC_, L, MEM, 4, 256).transpose(1, 0, 2, 3, 4))
    p_mv = np.ascontiguousarray(st("o_mv").reshape(NC_, L, MEM, 4, 256).transpose(1, 0, 2, 3, 4))
    s_s = np.ascontiguousarray(st("o_ss").transpose(1, 0, 2, 3, 4, 5)).reshape(L, NC_ * NSEQ, 4, 128, 128)
    s_q = np.ascontiguousarray(st("o_qs").reshape(NC_, L, NSEQ, 3, 1536).transpose(1, 0, 2, 3, 4)).reshape(L, NC_ * NSEQ, 3, 1536)
    s_c = np.ascontiguousarray(st("o_cs").reshape(NC_, L, NSEQ, 2, 512).transpose(1, 0, 2, 3, 4)).reshape(L, NC_ * NSEQ, 2, 512)
    return (yp, ys, p_s, p_q, p_c, p_mk, p_mv, s_s, s_q, s_c)
```

```python
import numpy as np
from contextlib import ExitStack
import concourse.bass as bass
import concourse.mybir as mybir
from concourse.bass_utils import run_bass_kernel_spmd

F32 = mybir.dt.float32
BF16 = mybir.dt.bfloat16
AF = mybir.ActivationFunctionType
ALU = mybir.AluOpType

ENGINES = ("tensor", "vector", "scalar", "gpsimd", "sync")

D = 1024
DFF = 2816
NFC = DFF // 128
INC = 3592
OFF_Z, OFF_BETA, OFF_A, OFF_SC = 1536, 2048, 2052, 2056
MEM = 256
NSEQ = 16
LS = 8
NEG = -30000.0


class Sem:
    def __init__(self, name):
        self.name = name
        self.total = 0
        self.handle = None
        self.is_dma = name.startswith("D_")


class Buf:
    __slots__ = ("name", "w", "r", "excl")

    def __init__(self, name="", excl=False):
        self.name = name
        self.w = None
        self.r = {}
        self.excl = excl


class Prog:
    def __init__(self, dry=False):
        self.dry = dry
        self.ops = {e: [] for e in ENGINES}
        self.esem = {e: Sem("E_" + e) for e in ENGINES}
        self.sems = list(self.esem.values())
        self.seen = {e: {} for e in ENGINES}

    def new_sem(self, name):
        s = Sem(name)
        self.sems.append(s)
        return s

    def _waits(self, eng, reads, writes, extra=()):
        need = {}
        for b in reads:
            if b.w is not None:
                s, v = b.w
                if need.get(s, 0) < v:
                    need[s] = v
        for b in writes:
            if b.w is not None:
                s, v = b.w
                if need.get(s, 0) < v:
                    need[s] = v
            for s, v in b.r.items():
                if need.get(s, 0) < v:
                    need[s] = v
        for s, v in extra:
            if need.get(s, 0) < v:
                need[s] = v
        out = []
        seen = self.seen[eng]
        pe = self.esem["tensor"]
        for s, v in need.items():
            if eng == "tensor" and s is pe:
                continue
            if s.is_dma:
                v = s.total
            if seen.get(s, 0) >= v:
                continue
            seen[s] = v
            out.append((s, v))
        return out

    def _mark(self, t, reads, writes):
        s, v = t
        for b in reads:
            if b.r.get(s, 0) < v:
                b.r[s] = v
        for b in writes:
            b.w = t
            b.r = {}

    def op(self, eng, fn, reads=(), writes=()):
        if self.dry:
            return None
        ex = [b for b in reads if b.excl]
        if ex:
            writes = list(writes) + ex
        waits = self._waits(eng, reads, writes)
        s = self.esem[eng]
        s.total += 1
        t = (s, s.total)
        self.ops[eng].append((fn, waits, (s, 1, s.total)))
        self._mark(t, reads, writes)
        return t

    def dma(self, eng, sem, fn, reads=(), writes=()):
        if self.dry:
            return None
        waits = self._waits(eng, reads, writes)
        sem.total += 16
        t = (sem, sem.total)
        self.ops[eng].append((fn, waits, (sem, 16, sem.total)))
        self._mark(t, reads, writes)
        return t

    def wait(self, eng, tickets):
        waits = self._waits(eng, (), (), tickets)
        if waits:
            self.ops[eng].append((None, waits, None))

    def replay(self, block):
        needed = {}
        for e in ENGINES:
            for fn, waits, inc in self.ops[e]:
                for s, v in waits:
                    if not s.is_dma:
                        needed.setdefault(s, set()).add(v)
        rank = {}
        for s, vs in needed.items():
            rank[s] = {v: i + 1 for i, v in enumerate(sorted(vs))}

        def run(e):
            def body(h):
                for fn, waits, inc in self.ops[e]:
                    for s, v in waits:
                        h.wait_ge(s.handle, v if s.is_dma else rank[s][v])
                    if fn is None:
                        continue
                    ins = fn(h)
                    if inc is not None:
                        s, amt, val = inc
                        if s.is_dma or val in needed.get(s, ()):
                            ins.then_inc(s.handle, amt)
            return body
        block.tensor(run("tensor"))
        block.vector(run("vector"))
        block.scalar(run("scalar"))
        block.gpsimd(run("gpsimd"))
        block.sync(run("sync"))


class Ring:
    def __init__(self, tiles):
        self.tiles = tiles
        self.bufs = [Buf() for _ in tiles]
        self.i = 0

    def next(self):
        k = self.i % len(self.tiles)
        self.i += 1
        return self.tiles[k], self.bufs[k]


def make_blocks(seq):
    if seq == 2048:
        gl = [[(0, 512, 'p'), (512, 256, 'p')], [(768, 256, 'p'), (1024, 512, 'p')],
              [(1536, 512, 'p'), (2048, 128, 's')]]
    else:
        assert seq % 128 == 0 and seq <= 512
        gl = [[(0, seq, 'p')], [(seq, 128, 's')]] if seq <= 128 else \
             [[(0, seq // 2, 'p')], [(seq // 2, seq // 2, 'p'), (seq, 128, 's')]]
    blocks = []
    for g in gl:
        lo = 0
        grp = []
        for (t0, n, kind) in g:
            grp.append((t0, n, kind, lo))
            lo += n
        blocks.append(grp)
    return blocks


class Builder:
    def __init__(self, nc, L, SEQ, dry, log=None):
        self.nc, self.L, self.SEQ = nc, L, SEQ
        self.NT = SEQ + 128
        self.blocks = make_blocks(SEQ)
        self.TBM = max(sum(g[1] for g in b) for b in self.blocks)
        self.NTM = self.TBM // 128
        self.P = Prog(dry)
        self.dry = dry
        self.slab_log = [] if log is None else log
        self.slab_i = 0
        self.slab_emitted = 0
        self.kv_log_i = 0

    def mm(self, out, lhsT, rhs, start, stop, r, w):
        self.P.op("tensor", lambda h: h.matmul(out, lhsT=lhsT, rhs=rhs, start=start, stop=stop), r, w)

    def tr(self, out, in_, ident, r, w):
        self.P.op("tensor", lambda h: h.transpose(out=out, in_=in_, identity=ident), r, w)

    def act(self, out, in_, func, r, w, scale=None, bias=None):
        kw = {}
        if scale is not None:
            kw["scale"] = scale
        if bias is not None:
            kw["bias"] = bias
        self.P.op("scalar", lambda h: h.activation(out=out, in_=in_, func=func, **kw), r, w)

    def tt(self, eng, out, in0, in1, op, r, w):
        self.P.op(eng, lambda h: h.tensor_tensor(out=out, in0=in0, in1=in1, op=op), r, w)

    def ts(self, eng, out, in0, s1, op0, r, w, s2=None, op1=None):
        if op1 is None:
            self.P.op(eng, lambda h: h.tensor_scalar(out=out, in0=in0, scalar1=s1, scalar2=None, op0=op0), r, w)
        else:
            self.P.op(eng, lambda h: h.tensor_scalar(out=out, in0=in0, scalar1=s1, scalar2=s2, op0=op0, op1=op1), r, w)

    def stt(self, out, in0, scalar, in1, op0, op1, r, w):
        self.P.op("vector", lambda h: h.scalar_tensor_tensor(out=out, in0=in0, scalar=scalar, in1=in1,
                                                             op0=op0, op1=op1), r, w)

    def cp(self, eng, out, in_, r, w):
        if eng == "scalar":
            self.P.op("scalar", lambda h: h.activation(out=out, in_=in_, func=AF.Copy), r, w)
        else:
            self.P.op(eng, lambda h: h.tensor_copy(out=out, in_=in_), r, w)

    def recip(self, out, in_, r, w):
        self.P.op("vector", lambda h: h.reciprocal(out=out, in_=in_), r, w)

    def memset(self, eng, out, val, r, w):
        self.P.op(eng, lambda h: h.memset(out, val), r, w)

    def dma(self, q, sem, out, in_, r, w):
        return self.P.dma(q, sem, lambda h: h.dma_start(out=out, in_=in_), r, w)

    def sb(self, name, shape, dt):
        return self.es.enter_context(self.nc.sbuf_tensor(name, list(shape), dt))

    def ring_sb(self, name, shape, dt, n):
        return Ring([self.sb(f"{name}{i}", shape, dt) for i in range(n)])

    def slab(self, wname, l, k0, KC, ranges, live=1):
        ranges = tuple(ranges)
        ncol = sum(n for _, n in ranges)
        assert KC * ncol <= self.SLABE
        if self.dry:
            self.slab_log.append((wname, l, k0, KC, ranges))
            return self.slab_tiles[0][:, :KC * ncol].rearrange("p (k n) -> p k n", n=ncol), self.slab_bufs[0]
        i = self.slab_i
        self.slab_i += 1
        oldest_live = i - (live - 1)
        lim = min(len(self.slab_log), i + 1 + self.PF)
        while self.slab_emitted < lim and (self.slab_emitted <= i or self.slab_emitted - self.NSLAB < oldest_live):
            j = self.slab_emitted
            assert j - self.NSLAB < oldest_live, "slab ring too small for live set"
            wn2, l2, k02, kc2, rg2 = self.slab_log[j]
            n2 = sum(n for _, n in rg2)
            k = j % self.NSLAB
            dst = self.slab_tiles[k][:, :kc2 * n2].rearrange("p (k n) -> p k n", n=n2)
            src = self.W[wn2][l2].rearrange("(k p) n -> p k n", p=128)
            o = 0
            for (c2, nn) in rg2:
                self.dma("gpsimd", self.slab_sems[k], dst[:, :, o:o + nn], src[:, k02:k02 + kc2, c2:c2 + nn], [], [self.slab_bufs[k]])
                o += nn
            self.slab_emitted += 1
        k = i % self.NSLAB
        assert self.slab_log[i] == (wname, l, k0, KC, ranges), (self.slab_log[i], wname, l, k0, KC, ranges)
        return self.slab_tiles[k][:, :KC * ncol].rearrange("p (k n) -> p k n", n=ncol), self.slab_bufs[k]

    def build(self):
        nc, L, SEQ, NT, TBM, NTM = self.nc, self.L, self.SEQ, self.NT, self.TBM, self.NTM
        P = self.P
        dram = lambda n, s, k: nc.dram_tensor(n, list(s), F32, kind=k).ap()
        I, O = "ExternalInput", "ExternalOutput"
        self.xp = dram("xp", [SEQ, D], I)
        self.xs = dram("xs", [128, D], I)
        self.mem = dram("mem", [MEM, D], I)
        self.sgdn = dram("sgdn", [L, NSEQ, 4, 128, 128], I)
        self.sqkv = dram("sqkv", [L, NSEQ * 3, 1536], I)
        self.ssc = dram("ssc", [L, NSEQ * 2, 512], I)
        self.cK = dram("cK", [L, NSEQ, MEM, D], I)
        self.cV = dram("cV", [L, NSEQ, MEM, D], I)
        self.W = {"gu1": dram("w_ffn1_gu", [L, D, 2 * DFF], I), "gu2": dram("w_ffn2_gu", [L, D, 2 * DFF], I),
                  "dn1": dram("w_ffn1_down", [L, DFF, D], I), "dn2": dram("w_ffn2_down", [L, DFF, D], I),
                  "in": dram("w_in", [L, D, INC], I), "out": dram("w_out", [L, D, D], I),
                  "xq": dram("w_xq", [L, D, D], I), "xk": dram("w_xk", [L, D, D], I),
                  "xv": dram("w_xv", [L, D, D], I), "xo": dram("w_xo", [L, D, D], I)}
        self.NV = L * 32 + 8 + L * 48 + L * 12 + L + 4
        self.cvec_d = dram("cvec", [128, self.NV], I)
        self.crow_d = dram("crow", [128, 2 * L * 6 * 4], I)
        self.cmat_d = dram("cmat", [128, 9 * 128 + 16], I)
        self.yp = dram("yp", [SEQ, D], O)
        self.ys = dram("ys", [128, D], O)
        self.o_sp = dram("o_sp", [L, 4, 128, 128], O)
        self.o_qp = dram("o_qp", [L, 3 * 12, 128], O)
        self.o_cp = dram("o_cp", [L, 2 * 4, 128], O)
        self.o_mk = dram("o_mk", [L, MEM, D], O)
        self.o_mv = dram("o_mv", [L, MEM, D], O)
        self.o_ss = dram("o_ss", [L, NSEQ, 4, 128, 128], O)
        self.o_qs = dram("o_qs", [L, NSEQ * 3, 1536], O)
        self.o_cs = dram("o_cs", [L, NSEQ * 2, 512], O)

        with ExitStack() as es:
            self.es = es
            sb = self.sb
            self.xT = sb("xT", [128, 8, NT], F32)
            self.XT = {}
            for b in self.blocks:
                for (t0, n, kind, lo) in b:
                    self.XT[t0] = Buf("xT%d" % t0)
            self.xn = sb("xn", [128, 8, TBM], BF16)
            self.XN = [Buf("xn%d" % i) for i in range(4)]
            self.cvec = sb("cvec_s", [128, self.NV], F32)
            self.crow = sb("crow_s", [128, 2 * L * 6 * 4], F32)
            self.cmat = sb("cmat_s", [128, 9 * 128 + 16], F32)
            self.CONST = Buf("const")
            self.identb = sb("identb", [128, 128], BF16)
            self.onesb = sb("onesb", [128, 128], BF16)
            self.nega = sb("nega", [128, L * 6 * 4], F32)
            self.memT = sb("memT", [128, 8, MEM], BF16)
            self.MEMT = Buf("memT")
            OVB = max(NFC * TBM * 2, 1)
            mixb = (4 + 16) * TBM * 2
            attb = (16 * TBM + 4096) * 2
            self.ovl = sb("ovl", [128, max(OVB, mixb, attb) // 2], BF16)
            self.ov_cur = []
            self.so_cur = []
            self.NSLAB, self.PF, self.SLABE = 4, 3, 2048
            self.slab_tiles = [sb(f"slab{i}", [128, self.SLABE], BF16) for i in range(self.NSLAB)]
            self.slab_bufs = [Buf() for _ in range(self.NSLAB)]
            self.slab_sems = [P.new_sem(f"D_slab{i}") for i in range(self.NSLAB)]
            self.r_f512 = self.ring_sb("f512_", [128, 512], F32, 4)
            self.r_ub = self.ring_sb("ub", [128, 520], BF16, 2)
            self.r_dw = self.ring_sb("dw", [128, 4, 128], BF16, 2)
            self.r_b512 = self.ring_sb("b512_", [128, 512], BF16, 3)
            self.r_f128 = self.ring_sb("f128_", [128, 128], F32, 6)
            self.r_b128 = self.ring_sb("b128_", [128, 128], BF16, 28)
            self.r_stage = self.ring_sb("stage", [128, 1024], F32, 1)
            self.stage_sems = [P.new_sem(f"D_stage{i}") for i in range(1)]
            self.hb = []
            for h in range(4):
                d_ = {}
                for nm in ("eg", "oi"):
                    d_[nm] = (sb(f"h{h}{nm}", [128, 128], F32), Buf(f"h{h}{nm}"))
                for nm in ("qkd", "qdec", "kbg", "ktl", "vb", "nw", "vn"):
                    d_[nm] = (sb(f"h{h}{nm}", [128, 128], BF16), Buf(f"h{h}{nm}"))
                self.hb.append(d_)
            self.r_stage.sems = self.stage_sems
            banks = [es.enter_context(nc.psum_tensor(f"pb{i}", [128, 512], F32)) for i in range(8)]
            self.r_big = Ring(banks)
            for b_ in self.r_big.bufs:
                b_.excl = True
            self.r_big4 = Ring(banks[0:4])
            self.r_big4.bufs = self.r_big.bufs[0:4]
            smalls, sbufs = [], []
            for i in range(4, 8):
                for q in range(4):
                    smalls.append(banks[i][:, q * 128:(q + 1) * 128])
                    sbufs.append(self.r_big.bufs[i])
            self.r_small = Ring(smalls)
            self.r_small.bufs = sbufs
            self.hps = []
            for h in range(4):
                r_ = Ring(smalls[4 * h:4 * h + 4])
                r_.bufs = sbufs[4 * h:4 * h + 4]
                self.hps.append(r_)
            self.S = [sb(f"S{h}", [128, 128], F32) for h in range(4)]
            self.Sb = [sb(f"Sb{h}", [128, 128], BF16) for h in range(4)]
            self.SB = [Buf(f"S{h}") for h in range(4)]
            self.carq = sb("carq", [128, 3, 12], F32)
            self.CARQ = [Buf() for _ in range(12)]
            self.carc = sb("carc", [128, 2, 4], F32)
            self.CARC = [Buf() for _ in range(4)]
            self.stq = sb("stq", [128, 12, NSEQ, 3], F32)
            self.STQ = [Buf() for _ in range(12)]
            self.stc = sb("stc", [128, 4, NSEQ, 2], F32)
            self.STC = [Buf() for _ in range(4)]
            self.gcols = sb("gcols", [128, 6, NTM, 4], F32)
            self.GCOL = Buf("gcols")
            self.sovl = sb("sovl", [128, 10240], BF16)
            so = self.sovl
            self.ssamp = so[:, 0:4096].bitcast(F32).rearrange("p (s v) -> p s v", v=128)
            self.ssampb = so[:, 4096:6144].rearrange("p (s v) -> p s v", v=128)
            self.wz = so[:, 6144:8192].rearrange("p (s v) -> p s v", v=128)
            self.ktz = so[:, 8192:10240].rearrange("p (s v) -> p s v", v=128)
            self.kvs_t = [so[:, i * 4096:(i + 1) * 4096].rearrange("p (a t d) -> p a t d", a=2, t=2) for i in range(2)]
            self.kvs_sems = [P.new_sem(f"D_kvs{i}") for i in range(2)]
            self.kts_t = so[:, 8192:10240].rearrange("p (c m) -> p c m", m=MEM)
            self.KT = self.ovl[:, 16 * TBM:16 * TBM + 2048].rearrange("p (c m) -> p c m", m=MEM)
            self.Vb = self.ovl[:, 16 * TBM + 2048:16 * TBM + 4096].rearrange("p (t d) -> p t d", d=D)
            self.sem_in = P.new_sem("D_in")
            self.sem_out = P.new_sem("D_out")
            self.sem_ss = P.new_sem("D_ss")
            for s in P.sems:
                s.handle = es.enter_context(nc.semaphore(s.name))

            self.emit()

            if not self.dry:
                P.wait("sync", [(s, s.total) for s in P.sems if s.is_dma and s.total > 0])
                block = es.enter_context(nc.Block())
                P.replay(block)
        return nc

    def gain(self, l, which, c):
        o = (l * 4 + which) * 8 + c
        return self.cvec[:, o:o + 1]

    def gfin(self, c):
        o = self.L * 32 + c
        return self.cvec[:, o:o + 1]

    def cqw(self, l, c, j):
        o = self.L * 32 + 8 + (l * 12 + c) * 4 + j
        return self.cvec[:, o:o + 1]

    def scw(self, l, c, j):
        o = self.L * 32 + 8 + self.L * 48 + (l * 4 + c) * 3 + j
        return self.cvec[:, o:o + 1]

    def ggdn(self, l):
        o = self.L * 32 + 8 + self.L * 48 + self.L * 12 + l
        return self.cvec[:, o:o + 1]

    def cst(self, k):
        o = self.L * 32 + 8 + self.L * 48 + self.L * 12 + self.L + k
        return self.cvec[:, o:o + 1]

    def cm(self, k):
        return self.cmat[:, k * 128:(k + 1) * 128]

    def emit(self):
        L = self.L
        C = self.CONST
        q = "sync"
        self.dma(q, self.sem_in, self.cvec[:], self.cvec_d[:, :], [], [C])
        self.dma(q, self.sem_in, self.crow[:], self.crow_d[:, :], [], [C])
        self.dma(q, self.sem_in, self.cmat[:], self.cmat_d[:, :], [], [C])
        self.cp("vector", self.identb[:], self.cm(0), [C], [C])
        self.memset("vector", self.onesb[:], 1.0, [], [C])
        n24 = L * 24
        self.act(self.nega[:], self.crow[:, n24:2 * n24], AF.Exp, [C], [C])
        self.ts("vector", self.nega[:], self.nega[:], -1.0, ALU.mult, [C], [C])
        self.load_x()
        self.load_mem()
        import os
        ks = os.environ.get("KSTOP", "full")
        for l in range(L):
            for bi, blk in enumerate(self.blocks):
                if ks == "none":
                    continue
                self.ffn(l, 1, blk)
                if ks == "ffn":
                    continue
                self.mix(l, bi, blk)
                if ks.startswith("mix"):
                    continue
                self.attn(l, bi, blk)
                self.ffn(l, 2, blk)
        self.final()

    def load_x(self):
        C = self.CONST
        ident = self.cm(0)
        ntile = self.NT // 128
        for ti in range(ntile):
            st, SB_ = self.r_stage.next()
            k = 0
            src = self.xp[ti * 128:(ti + 1) * 128, :] if ti * 128 < self.SEQ else self.xs[:, :]
            self.dma("sync", self.stage_sems[k], st[:], src, [], [SB_])
            t0 = ti * 128
            g0 = max(k_ for k_ in self.XT if k_ <= t0)
            for half in range(2):
                ps, PB = self.r_big.next()
                for c4 in range(4):
                    c = half * 4 + c4
                    self.tr(ps[:, c4 * 128:(c4 + 1) * 128], st[:, c * 128:(c + 1) * 128], ident, [SB_, C], [PB])
                self.cp("vector" if half else "scalar",
                        self.xT[:, half * 4:half * 4 + 4, t0:t0 + 128],
                        ps[:].rearrange("p (c n) -> p c n", n=128), [PB], [self.XT[g0]])

    def load_mem(self):
        C = self.CONST
        ident = self.cm(0)
        for mt in range(2):
            st, SB_ = self.r_stage.next()
            k = 0
            self.dma("sync", self.stage_sems[k], st[:], self.mem[mt * 128:(mt + 1) * 128, :], [], [SB_])
            for half in range(2):
                ps, PB = self.r_big.next()
                for c4 in range(4):
                    c = half * 4 + c4
                    self.tr(ps[:, c4 * 128:(c4 + 1) * 128], st[:, c * 128:(c + 1) * 128], ident, [SB_, C], [PB])
                self.cp("vector", self.memT[:, half * 4:half * 4 + 4, mt * 128:(mt + 1) * 128],
                        ps[:].rearrange("p (c n) -> p c n", n=128), [PB], [self.MEMT])

    def sumsq(self, X, t0, n):
        ps, PB = self.r_big.next()
        for c in range(8):
            sq, SQ = self.r_b512.next()
            self.act(sq[:, :n], self.xT[:, c, t0:t0 + n], AF.Square, [X], [SQ])
            self.mm(ps[:, :n], self.onesb[:], sq[:, :n], c == 0, c == 7, [SQ, self.CONST], [PB])
        rs, RS = self.r_f512.next()
        self.act(rs[:, :n], ps[:, :n], AF.Ln, [PB, self.CONST], [RS], scale=1.0 / D, bias=self.cst(0))
        self.act(rs[:, :n], rs[:, :n], AF.Exp, [RS], [RS], scale=-0.5)
        return rs, RS

    def norm(self, l, which, blk):
        for gi, (t0, n, kind, lo) in enumerate(blk):
            X = self.XT[t0]
            rs, RS = self.sumsq(X, t0, n)
            for c in range(8):
                self.stt(self.xn[:, c, lo:lo + n], self.xT[:, c, t0:t0 + n], self.gain(l, which, c), rs[:, :n],
                         ALU.mult, ALU.mult, [X, RS, self.CONST], [self.XN[gi]])

    def proj(self, sl, SL, KC, col, xin, XIN_g, gi, lo, n):
        ps, PB = self.r_big.next()
        for kc in range(KC):
            self.mm(ps[:, :n], sl[:, kc, col:col + 128], xin(kc, lo, n), kc == 0, kc == KC - 1, [SL, XIN_g], [PB])
        return ps, PB

    def xn_in(self, kc, lo, n):
        return self.xn[:, kc, lo:lo + n]

    def phase(self, names):
        merged = {}
        for b in self.ov_cur:
            if b.w is not None:
                s_, v = b.w
                merged[s_] = max(merged.get(s_, 0), v)
            for s_, v in b.r.items():
                merged[s_] = max(merged.get(s_, 0), v)
        new = []
        for nm in names:
            b = Buf(nm)
            b.r = dict(merged)
            new.append(b)
        self.ov_cur = new
        return new

    def ffn(self, l, which, blk):
        self.norm(l, 0 if which == 1 else 3, blk)
        TBM = self.TBM
        hT = self.ovl[:, :NFC * TBM].rearrange("p (c n) -> p c n", n=TBM)
        HB = self.phase(["h%d" % i for i in range(len(blk))])
        gu, dn = "gu%d" % which, "dn%d" % which
        for fp in range(NFC // 2):
            gsl, GSL = self.slab(gu, l, 0, 8, [(fp * 256, 256)])
            usl, USL = self.slab(gu, l, 0, 8, [(DFF + fp * 256, 256)], live=2)
            for i in range(2):
                fc = fp * 2 + i
                for gi, (t0, n, kind, lo) in enumerate(blk):
                    pg, PG = self.proj(gsl, GSL, 8, i * 128, self.xn_in, self.XN[gi], gi, lo, n)
                    pu, PU = self.proj(usl, USL, 8, i * 128, self.xn_in, self.XN[gi], gi, lo, n)
                    sg, SG = self.r_f512.next()
                    self.act(sg[:, :n], pg[:, :n], AF.Silu, [PG], [SG])
                    self.tt("vector", hT[:, fc, lo:lo + n], sg[:, :n], pu[:, :n], ALU.mult, [SG, PU], [HB[gi]])
        fparts = [(0, 8), (8, 8), (16, NFC - 16)]
        for dp in range(4):
            accs = [[self.r_big.next() for _ in blk] for _ in range(2)]
            for pi, (f0, nf) in enumerate(fparts):
                dsl, DSL = self.slab(dn, l, f0, nf, [(dp * 256, 256)])
                for i in range(2):
                    for gi, (t0, n, kind, lo) in enumerate(blk):
                        py, PY = accs[i][gi]
                        for kc in range(nf):
                            self.mm(py[:, :n], dsl[:, kc, i * 128:(i + 1) * 128], hT[:, f0 + kc, lo:lo + n],
                                    pi == 0 and kc == 0, pi == 2 and kc == nf - 1, [DSL, HB[gi]], [PY])
            for i in range(2):
                dc = dp * 2 + i
                for gi, (t0, n, kind, lo) in enumerate(blk):
                    py, PY = accs[i][gi]
                    X = self.XT[t0]
                    self.stt(self.xT[:, dc, t0:t0 + n], py[:, :n], 0.5, self.xT[:, dc, t0:t0 + n], ALU.mult, ALU.add,
                             [PY, X], [X])

    def mix(self, l, bi, blk):
        TBM, L = self.TBM, self.L
        C = self.CONST
        self.norm(l, 1, blk)
        ntile = sum(g[1] for g in blk) // 128
        has_s = blk[-1][2] == 's'
        last_p = (bi == len(self.blocks) - 1)
        ov = self.ovl
        osc = ov[:, 0:4 * TBM].rearrange("p (c n) -> p c n", n=TBM)
        qkvz = ov[:, 4 * TBM:20 * TBM].rearrange("p (c n) -> p c n", n=TBM)
        OSC, QKVZ = self.phase(["osc", "qkvz"])
        if bi == 0:
            for c in range(12):
                self.memset("vector", self.carq[:, :, c], 0.0, [], [self.CARQ[c]])
            for c in range(4):
                self.memset("vector", self.carc[:, :, c], 0.0, [], [self.CARC[c]])
            for h in range(4):
                self.memset("vector", self.S[h][:], 0.0, [], [self.SB[h]])
                self.memset("vector", self.Sb[h][:], 0.0, [], [self.SB[h]])
        if has_s:
            self.load_sample_conv_state(l)
        import os
        ks = os.environ.get("KSTOP", "full")
        if ks == "mix_0":
            return
        self.gates(l, blk, ntile, has_s)
        if ks == "mix_a":
            return
        for c in range(4):
            if c % 2 == 0:
                bsl, BSL = self.slab("in", l, 0, 8, [(OFF_SC + c * 128, 256)])
                csl, CSL = self.slab("in", l, 0, 8, [(OFF_SC + 512 + c * 128, 256)], live=2)
                hsl, HSL = self.slab("in", l, 0, 8, [(OFF_SC + 1024 + c * 128, 256)], live=3)
            co = (c % 2) * 128
            dw, DW = self.diagw(l, c, 3, self.scw)
            for gi, (t0, n, kind, lo) in enumerate(blk):
                pc, PC = self.proj(csl, CSL, 8, co, self.xn_in, self.XN[gi], gi, lo, n)
                ph, PH = self.proj(hsl, HSL, 8, co, self.xn_in, self.XN[gi], gi, lo, n)
                pb, PBB = self.proj(bsl, BSL, 8, co, self.xn_in, self.XN[gi], gi, lo, n)
                cs, CS = self.r_f512.next()
                self.cp("scalar", cs[:, :n], pc[:, :n], [PC], [CS])
                bs, BS = self.r_f512.next()
                self.cp("scalar", bs[:, :n], pb[:, :n], [PBB], [BS])

                def fill(dst, r, w, cs=cs, ph=ph, n=n, kind=kind, CS=CS, PH=PH):
                    self.tt("vector", dst, cs_view(cs, n, kind), ph_view(ph, n, kind), ALU.mult, r + [CS, PH], w)

                def tail(dst, r, w, cs=cs, ph=ph, n=n, kind=kind, CS=CS, PH=PH):
                    if kind == 'p':
                        self.tt("vector", dst, cs[:, n - 2:n], ph[:, n - 2:n], ALU.mult, r + [CS, PH], w)
                    else:
                        self.tt("vector", dst, cs_view(cs, n, kind)[:, :, LS - 2:LS], ph_view(ph, n, kind)[:, :, LS - 2:LS],
                                ALU.mult, r + [CS, PH], w)
                pcv, PCV = self.conv(c, n, kind, 2, fill, tail, self.carc, self.CARC, self.stc, self.STC, dw, DW)
                self.tt("vector", osc[:, c, lo:lo + n], pcv[:, :n], bs[:, :n], ALU.mult, [PCV, BS], [OSC])
        if ks == "mix_b":
            return
        for part in range(4):
            for hp in range(2):
                sl, SL = self.slab("in", l, 0, 8, [(part * 512 + hp * 256, 256)])
                dws = [self.diagw(l, part * 4 + hp * 2 + h2, 4, self.cqw) for h2 in range(2)] if part < 3 else [None, None]
                for gi, grp in enumerate(blk):
                    gens = [self.qkv_g(l, part, hp * 2 + h2, h2, sl, SL, gi, grp, dws[h2], qkvz, QKVZ) for h2 in range(2)]
                    while gens:
                        for g_ in list(gens):
                            try:
                                next(g_)
                            except StopIteration:
                                gens.remove(g_)
        if ks == "mix_c":
            return
        if last_p:
            self.out_conv_state_prompt(l)
        if has_s:
            self.out_conv_state_sample(l)
        if ks == "mix_d":
            return
        if has_s:
            self.SSAMP, self.WZ, self.KTZ = self.sphase(["ssamp", "wz", "ktz"])
            self.memset("vector", self.wz[:], 0.0, [], [self.WZ])
        ti = 0
        for gi, (t0, n, kind, lo) in enumerate(blk):
            for tt_ in range(n // 128):
                c0 = lo + tt_ * 128
                if kind == 's':
                    for h in range(4):
                        self.gdn_tile(l, h, ti, c0, kind, qkvz, QKVZ, gi)
                else:
                    gens = [self.gdn_gen(l, h, ti, c0, qkvz, QKVZ, gi) for h in range(4)]
                    while gens:
                        for g_ in list(gens):
                            try:
                                next(g_)
                            except StopIteration:
                                gens.remove(g_)
                ti += 1
        if last_p:
            for h in range(4):
                self.dma("sync", self.sem_out, self.o_sp[l, h], self.S[h][:], [self.SB[h]], [])
        for s4 in range(4):
            osl, OSL = self.slab("out", l, 0, 8, [(s4 * 256, 256)])
            for i in range(2):
                dc = s4 * 2 + i
                for gi, (t0, n, kind, lo) in enumerate(blk):
                    ps, PB = self.r_big.next()
                    for kc in range(8):
                        rhs = self.xn[:, kc, lo:lo + n] if kc < 4 else osc[:, kc - 4, lo:lo + n]
                        self.mm(ps[:, :n], osl[:, kc, i * 128:(i + 1) * 128], rhs, kc == 0, kc == 7,
                                [OSL, self.XN[gi], OSC], [PB])
                    X = self.XT[t0]
                    self.tt("vector", self.xT[:, dc, t0:t0 + n], ps[:, :n], self.xT[:, dc, t0:t0 + n], ALU.add, [PB, X], [X])

    def sphase(self, names):
        merged = {}
        for b in self.so_cur:
            if b.w is not None:
                s_, v = b.w
                merged[s_] = max(merged.get(s_, 0), v)
            for s_, v in b.r.items():
                merged[s_] = max(merged.get(s_, 0), v)
        new = []
        for nm in names:
            b = Buf(nm)
            b.r = dict(merged)
            new.append(b)
        self.so_cur = new
        return new

    def diagw(self, l, c, taps, wfn):
        dw, DW = self.r_dw.next()
        for j in range(taps):
            self.ts("vector", dw[:, j, :], self.identb[:], wfn(l, c, j), ALU.mult, [self.CONST], [DW])
        return dw, DW

    def conv(self, c, n, kind, halo, fill, tail, car, CAR, st, ST, dw, DW):
        out = []
        for _ in self.conv_g(out, c, n, kind, halo, fill, tail, car, CAR, st, ST, dw, DW):
            pass
        return out[0]

    def conv_g(self, out, c, n, kind, halo, fill, tail, car, CAR, st, ST, dw, DW):
        taps = halo + 1
        ub, UB = self.r_ub.next()
        pc, PC = self.r_big.next()
        if kind == 'p':
            self.cp("vector", ub[:, 0:halo], car[:, :, c], [CAR[c]], [UB])
            fill(ub[:, halo:halo + n], [], [UB])
            tail(car[:, :, c], [], [CAR[c]])
            yield
            for j in range(taps):
                self.mm(pc[:, :n], dw[:, j, :], ub[:, j:j + n], j == 0, j == taps - 1, [DW, UB], [PC])
        else:
            w_ = LS + halo
            u3 = ub[:, :NSEQ * w_].rearrange("p (s t) -> p s t", t=w_)
            p3 = pc[:, :128].rearrange("p (s t) -> p s t", t=LS)
            self.cp("vector", u3[:, :, 0:halo], st[:, c, :, :], [ST[c]], [UB])
            fill(u3[:, :, halo:halo + LS], [], [UB])
            tail(st[:, c, :, :], [], [ST[c]])
            yield
            for j in range(taps):
                self.mm(p3, dw[:, j, :], u3[:, :, j:j + LS], j == 0, j == taps - 1, [DW, UB], [PC])
        out.append((pc, PC))

    def qkv_g(self, l, part, h, h2, sl, SL, gi, grp, dwp, qkvz, QKVZ):
        C = self.CONST
        (t0, n, kind, lo) = grp
        ps, PS = self.proj(sl, SL, 8, h2 * 128, self.xn_in, self.XN[gi], gi, lo, n)
        dst = qkvz[:, part * 4 + h, lo:lo + n]
        yield
        if part == 3:
            self.act(dst, ps[:, :n], AF.Silu, [PS], [QKVZ])
            return

        def fill(d_, r, w):
            self.cp("scalar", d_, ps_view(ps, n, kind), r + [PS], w)

        def tail(d_, r, w):
            if kind == 'p':
                self.cp("vector", d_, ps[:, n - 3:n], r + [PS], w)
            else:
                self.cp("vector", d_, ps_view(ps, n, kind)[:, :, LS - 3:LS], r + [PS], w)
        out = []
        for _ in self.conv_g(out, part * 4 + h, n, kind, 3, fill, tail, self.carq, self.CARQ, self.stq, self.STQ,
                             dwp[0], dwp[1]):
            yield
        acc, ACC = out[0]
        yield
        if part == 2:
            self.act(dst, acc[:, :n], AF.Silu, [ACC], [QKVZ])
            return
        qs, QS = self.r_f512.next()
        self.act(qs[:, :n], acc[:, :n], AF.Silu, [ACC], [QS])
        yield
        sq, SQ = self.r_b512.next()
        self.act(sq[:, :n], qs[:, :n], AF.Square, [QS], [SQ])
        yield
        p2, P2 = self.r_big.next()
        self.mm(p2[:, :n], self.onesb[:], sq[:, :n], True, True, [SQ, C], [P2])
        yield
        rn, RN = self.r_f512.next()
        self.act(rn[:, :n], p2[:, :n], AF.Ln, [P2, C], [RN], bias=self.cst(0))
        yield
        self.act(rn[:, :n], rn[:, :n], AF.Exp, [RN], [RN], scale=-0.5)
        yield
        self.stt(dst, qs[:, :n], (128.0 ** -0.5) if part == 0 else 1.0, rn[:, :n], ALU.mult, ALU.mult,
                 [QS, RN], [QKVZ])

    def load_sample_conv_state(self, l):
        C = self.CONST
        ident = self.cm(0)
        st, SB_ = self.r_stage.next()
        k = 0
        self.dma("sync", self.stage_sems[k], st[:48, :], self.sqkv[l][:, 0:1024], [], [SB_])
        for grp in range(3):
            if grp == 2:
                st, SB_ = self.r_stage.next()
                k = 0
                self.dma("sync", self.stage_sems[k], st[:48, 0:512], self.sqkv[l][:, 1024:1536], [], [SB_])
                self.dma("sync", self.stage_sems[k], st[:32, 512:1024], self.ssc[l][:, :], [], [SB_])
            ps, PB = self.r_big.next()
            for c4 in range(4):
                off = (c4 if grp == 2 else grp * 4 + c4) * 128
                self.tr(ps[:, c4 * 128:c4 * 128 + 48], st[:48, off:off + 128], ident[:48, :48], [SB_, C], [PB])
            for c4 in range(4):
                c = grp * 4 + c4
                self.cp("vector", self.stq[:, c, :, :], ps[:, c4 * 128:c4 * 128 + 48].rearrange("p (s r) -> p s r", r=3),
                        [PB], [self.STQ[c]])
        ps, PB = self.r_big.next()
        for c in range(4):
            self.tr(ps[:, c * 128:c * 128 + 32], st[:32, 512 + c * 128:512 + (c + 1) * 128], ident[:32, :32], [SB_, C], [PB])
        for c in range(4):
            self.cp("vector", self.stc[:, c, :, :], ps[:, c * 128:c * 128 + 32].rearrange("p (s r) -> p s r", r=2),
                    [PB], [self.STC[c]])

    def out_conv_state_prompt(self, l):
        C = self.CONST
        ident = self.cm(0)
        ps, PB = self.r_big.next()
        self.tr(ps[:36, 0:128], self.carq[:].rearrange("p r c -> p (r c)"), ident, self.CARQ + [C], [PB])
        self.tr(ps[:8, 128:256], self.carc[:].rearrange("p r c -> p (r c)"), ident, self.CARC + [C], [PB])
        st, SB_ = self.r_stage.next()
        k = 0
        self.cp("vector", st[:36, 0:128], ps[:36, 0:128], [PB], [SB_])
        self.cp("vector", st[:8, 128:256], ps[:8, 128:256], [PB], [SB_])
        self.dma("sync", self.stage_sems[k], self.o_qp[l], st[:36, 0:128], [SB_], [])
        self.dma("sync", self.stage_sems[k], self.o_cp[l], st[:8, 128:256], [SB_], [])

    def out_conv_state_sample(self, l):
        C = self.CONST
        ident = self.cm(0)
        for grp in range(3):
            ps, PB = self.r_big.next()
            for c4 in range(4):
                c = grp * 4 + c4
                self.tr(ps[:48, c4 * 128:(c4 + 1) * 128], self.stq[:, c, :, :].rearrange("p s r -> p (s r)"), ident,
                        [self.STQ[c], C], [PB])
            st, SB_ = self.r_stage.next()
            k = 0
            self.cp("vector", st[:48, 0:512], ps[:48, :], [PB], [SB_])
            self.dma("sync", self.stage_sems[k], self.o_qs[l][:, grp * 512:(grp + 1) * 512], st[:48, 0:512], [SB_], [])
        ps, PB = self.r_big.next()
        for c in range(4):
            self.tr(ps[:32, c * 128:(c + 1) * 128], self.stc[:, c, :, :].rearrange("p s r -> p (s r)"), ident,
                    [self.STC[c], C], [PB])
        st, SB_ = self.r_stage.next()
        k = 0
        self.cp("vector", st[:32, 0:512], ps[:32, :], [PB], [SB_])
        self.dma("sync", self.stage_sems[k], self.o_cs[l][:, :], st[:32, 0:512], [SB_], [])

    def gates(self, l, blk, ntile, has_s):
        C = self.CONST
        L = self.L
        gc = self.gcols
        G = self.GCOL
        wsl, WSL = self.slab("in", l, 0, 8, [(OFF_BETA, 8)])
        pba, PBA = self.r_small.next()
        pv = pba[:, :ntile * 8].rearrange("p (t k) -> p t k", k=8)
        ti = 0
        for gi, (t0, n, kind, lo) in enumerate(blk):
            for t_ in range(n // 128):
                c0 = lo + t_ * 128
                for kc in range(8):
                    self.mm(pv[:, ti, :], self.xn[:, kc, c0:c0 + 128], wsl[:, kc, :], kc == 0, kc == 7,
                            [self.XN[gi], WSL], [PBA])
                ti += 1
        tmp, TMP = self.r_f128.next()
        tb = tmp[:, 0:ntile * 4].rearrange("p (t k) -> p t k", k=4)
        ta = tmp[:, 64:64 + ntile * 4].rearrange("p (t k) -> p t k", k=4)
        nt4 = ntile * 4
        dtb = self.crow[:, l * 24:l * 24 + nt4].rearrange("p (t k) -> p t k", k=4)
        nga = self.nega[:, l * 24:l * 24 + nt4].rearrange("p (t k) -> p t k", k=4)
        self.act(tb, pv[:, :, 0:4], AF.Exp, [PBA], [TMP], scale=-1.0)
        self.act(tb, tb, AF.Ln, [TMP, C], [TMP], bias=self.cst(1))
        self.ts("vector", gc[:, 1, :ntile, :], tb, -1.0, ALU.mult, [TMP], [G])
        self.tt("vector", ta, pv[:, :, 4:8], dtb, ALU.add, [PBA, C], [TMP])
        self.act(ta, ta, AF.Exp, [TMP], [TMP])
        self.act(ta, ta, AF.Ln, [TMP, C], [TMP], bias=self.cst(1))
        self.tt("vector", gc[:, 0, :ntile, :], ta, nga, ALU.mult, [TMP, C], [G])
        npt = ntile - 1 if has_s else ntile
        pc, PC = self.r_small.next()
        pl, PL = self.r_small.next()
        pcv = pc[:, :nt4].rearrange("p (t k) -> p t k", k=4)
        plv = pl[:, :nt4].rearrange("p (t k) -> p t k", k=4)
        if npt > 0:
            self.mm(pcv[:, :npt, :], self.cm(1), gc[:, 0, :npt, :], True, True, [G, C], [PC])
            self.mm(plv[:, :npt, :], self.cm(2), gc[:, 0, :npt, :], True, True, [G, C], [PL])
        if has_s:
            self.mm(pcv[:, npt:ntile, :], self.cm(3), gc[:, 0, npt:ntile, :], True, True, [G, C], [PC])
            self.mm(plv[:, npt:ntile, :], self.cm(4), gc[:, 0, npt:ntile, :], True, True, [G, C], [PL])
        self.cp("vector", gc[:, 2, :ntile, :], pcv, [PC], [G])
        t2, T2 = self.r_f128.next()
        t2a = t2[:, 0:nt4].rearrange("p (t k) -> p t k", k=4)
        t2b = t2[:, 64:64 + nt4].rearrange("p (t k) -> p t k", k=4)
        self.tt("vector", t2a, pcv, gc[:, 1, :ntile, :], ALU.add, [PC, G], [T2])
        self.act(gc[:, 3, :ntile, :], t2a, AF.Exp, [T2], [G])
        self.tt("vector", t2b, plv, gc[:, 2, :ntile, :], ALU.subtract, [PL, G], [T2])
        self.act(gc[:, 4, :ntile, :], t2b, AF.Exp, [T2], [G])
        self.act(gc[:, 5, :ntile, :], gc[:, 1, :ntile, :], AF.Exp, [G], [G])

    def gdn_gen(self, l, h, ti, c0, qkvz, QKVZ, gi):
        C = self.CONST
        G = self.GCOL
        gc = self.gcols
        mi, ms, tri, identf = self.cm(5), self.cm(6), self.cm(1), self.cm(0)
        qT = qkvz[:, h, c0:c0 + 128]
        kT = qkvz[:, 4 + h, c0:c0 + 128]
        vT = qkvz[:, 8 + h, c0:c0 + 128]
        zs = qkvz[:, 12 + h, c0:c0 + 128]
        col = lambda k: gc[:, k, ti, h:h + 1]
        bc = lambda k: gc[:, k, ti, h:h + 1].broadcast_to([128, 128])
        S_, Sb_, SBF = self.S[h], self.Sb[h], self.SB[h]
        ps = self.hps[h]
        hb = self.hb[h]
        eg, EG = hb["eg"]
        oi, OI = hb["oi"]
        qkd, QKD = hb["qkd"]
        qdec, QDEC = hb["qdec"]
        kbg, KBG = hb["kbg"]
        ktl, KTL = hb["ktl"]
        vb, VB = hb["vb"]
        nw, NW = hb["nw"]
        vn, VN = hb["vn"]
        pg1, PB_ = ps.next()
        self.mm(pg1, bc(0), tri, True, True, [G, C], [PB_])
        pg2, _ = ps.next()
        self.mm(pg2, bc(0), tri, True, False, [G, C], [PB_])
        self.mm(pg2, bc(1), identf, False, True, [G, C], [PB_])
        pG, _ = ps.next()
        self.mm(pG, kT, kT, True, True, [QKVZ], [PB_])
        pQ, _ = ps.next()
        self.mm(pQ, kT, qT, True, True, [QKVZ], [PB_])
        yield
        d1, D1 = oi, OI
        self.stt(d1[:], pg1, col(2), mi, ALU.subtract, ALU.add, [PB_, G, C], [D1])
        d2, D2 = self.r_f128.next()
        self.stt(d2[:], pg2, col(2), ms, ALU.subtract, ALU.add, [PB_, G, C], [D2])
        self.act(eg[:], pg1, AF.Exp, [PB_], [EG])
        self.act(d1[:], d1[:], AF.Exp, [D1], [D1])
        self.act(d2[:], d2[:], AF.Exp, [D2], [D2])
        yield
        Bm, BM = self.r_b128.next()
        self.tt("vector", Bm[:], pG, d2[:], ALU.mult, [PB_, D2], [BM])
        self.tt("vector", qkd[:], pQ, d1[:], ALU.mult, [PB_, D1], [QKD])
        self.tt("vector", qdec[:], qT, eg[:], ALU.mult, [QKVZ, EG], [QDEC])
        yield
        pk_, _ = ps.next()
        pkt = pk_.bitcast(BF16)[:, 0:128]
        self.tr(pkt, kT, self.identb[:], [QKVZ, C], [PB_])
        pv_, _ = ps.next()
        pvt = pv_.bitcast(BF16)[:, 0:128]
        self.tr(pvt, vT, self.identb[:], [QKVZ, C], [PB_])
        pa_, _ = ps.next()
        pat = pa_.bitcast(BF16)[:, 0:128]
        self.tr(pat, Bm[:], self.identb[:], [BM, C], [PB_])
        yield
        self.act(kbg[:], pkt, AF.Copy, [PB_, G], [KBG], scale=col(3))
        self.ts("vector", ktl[:], pkt, col(4), ALU.mult, [PB_, G], [KTL])
        self.act(vb[:], pvt, AF.Copy, [PB_, G], [VB], scale=col(5))
        Am, AM = self.r_b128.next()
        self.cp("scalar", Am[:], pat, [PB_], [AM])
        Pm, PM = self.r_b128.next()
        self.tt("vector", Pm[:], self.identb[:], Bm[:], ALU.subtract, [C, BM], [PM])
        yield
        nlev = 5
        Ak, AK, Bk, BK = Am, AM, Bm, BM
        for lev in range(nlev):
            pa2, _ = ps.next()
            self.mm(pa2, Bk[:], Ak[:], True, True, [BK, AK], [PB_])
            if lev < nlev - 1:
                pb2, _ = ps.next()
                self.mm(pb2, Ak[:], Bk[:], True, True, [BK, AK], [PB_])
            yield
            An, AN = self.r_b128.next()
            self.cp("scalar", An[:], pa2, [PB_], [AN])
            if lev < nlev - 1:
                Bn, BN = self.r_b128.next()
                self.cp("vector", Bn[:], pb2, [PB_], [BN])
            yield
            pp, _ = ps.next()
            self.mm(pp, An[:], Pm[:], True, True, [AN, PM], [PB_])
            yield
            Pn, PN = self.r_b128.next()
            self.tt("vector", Pn[:], pp, Pm[:], ALU.add, [PB_, PM], [PN])
            Pm, PM = Pn, PN
            Ak, AK = An, AN
            if lev < nlev - 1:
                Bk, BK = Bn, BN
            yield
        TT, TTB = Pm, PM
        pw, _ = ps.next()
        self.mm(pw, kbg[:], TT[:], True, True, [KBG, TTB], [PB_])
        yield
        self.act(nw[:], pw, AF.Copy, [PB_], [NW], scale=-1.0)
        yield
        for c in range(2):
            sl = slice(64 * c, 64 * c + 64)
            po_i, _ = ps.next()
            self.mm(po_i[:, sl], Sb_[:], qdec[:, sl], True, True, [SBF, QDEC], [PB_])
            pvn, _ = ps.next()
            self.mm(pvn[sl, :], TT[:, sl], vb[:], True, False, [TTB, VB], [PB_])
            self.mm(pvn[sl, :], nw[:, sl], Sb_[:], False, True, [NW, SBF], [PB_])
            yield
            self.cp("scalar", vn[sl, :], pvn[sl, :], [PB_], [VN])
            self.cp("vector", oi[:, sl], po_i[:, sl], [PB_], [OI])
            yield
            pS, _ = ps.next()
            self.mm(pS, ktl[sl, :], vn[sl, :], True, True, [KTL, VN], [PB_])
            yield
            self.stt(S_[:], S_[:], eg[:, 64 * c + 63:64 * c + 64], pS, ALU.mult, ALU.add, [SBF, EG, PB_], [SBF])
            yield
            self.cp("scalar", Sb_[:], S_[:], [SBF], [SBF])
            yield
        po, _ = ps.next()
        self.mm(po, vn[:], qkd[:], True, True, [VN, QKD], [PB_])
        yield
        self.tt("vector", oi[:], po, oi[:], ALU.add, [PB_, OI], [OI])
        yield
        sq, SQ = self.r_b128.next()
        self.act(sq[:], oi[:], AF.Square, [OI], [SQ])
        yield
        pss, _ = ps.next()
        self.mm(pss, self.onesb[:], sq[:], True, True, [SQ, C], [PB_])
        yield
        rn, RN = self.r_f128.next()
        self.act(rn[:], pss, AF.Ln, [PB_, C], [RN], scale=1.0 / 128, bias=self.cst(0))
        yield
        self.act(rn[:], rn[:], AF.Exp, [RN], [RN], scale=-0.5)
        self.stt(oi[:], oi[:], self.ggdn(l), rn[:], ALU.mult, ALU.mult, [OI, RN, C], [OI])
        self.tt("vector", self.xn[:, h, c0:c0 + 128], oi[:], zs, ALU.mult, [OI, QKVZ], [self.XN[gi]])

    def gdn_tile(self, l, h, ti, c0, kind, qkvz, QKVZ, gi):
        C = self.CONST
        G = self.GCOL
        gc = self.gcols
        samp = (kind == 's')
        mi, ms = (self.cm(7), self.cm(8)) if samp else (self.cm(5), self.cm(6))
        tri = self.cm(3) if samp else self.cm(1)
        identf = self.cm(0)
        qT = qkvz[:, h, c0:c0 + 128]
        kT = qkvz[:, 4 + h, c0:c0 + 128]
        vT = qkvz[:, 8 + h, c0:c0 + 128]
        zs = qkvz[:, 12 + h, c0:c0 + 128]
        col = lambda k: gc[:, k, ti, h:h + 1]
        bc = lambda k: gc[:, k, ti, h:h + 1].broadcast_to([128, 128])
        S_, Sb_, SBF = self.S[h], self.Sb[h], self.SB[h]
        pg1, PG1 = self.r_small.next()
        self.mm(pg1, bc(0), tri, True, True, [G, C], [PG1])
        pg2, PG2 = self.r_small.next()
        self.mm(pg2, bc(0), tri, True, False, [G, C], [PG2])
        self.mm(pg2, bc(1), identf, False, True, [G, C], [PG2])
        pG, PGG = self.r_small.next()
        self.mm(pG, kT, kT, True, True, [QKVZ], [PGG])
        pQ, PQQ = self.r_small.next()
        self.mm(pQ, kT, qT, True, True, [QKVZ], [PQQ])
        pk_, PKT = self.r_small.next()
        pkt = pk_.bitcast(BF16)[:, 0:128]
        self.tr(pkt, kT, self.identb[:], [QKVZ, C], [PKT])
        pv_, PVT = self.r_small.next()
        pvt = pv_.bitcast(BF16)[:, 0:128]
        self.tr(pvt, vT, self.identb[:], [QKVZ, C], [PVT])
        d1, D1 = self.r_f128.next()
        self.stt(d1[:], pg1, col(2), mi, ALU.subtract, ALU.add, [PG1, G, C], [D1])
        self.act(d1[:], d1[:], AF.Exp, [D1], [D1])
        d2, D2 = self.r_f128.next()
        self.stt(d2[:], pg2, col(2), ms, ALU.subtract, ALU.add, [PG2, G, C], [D2])
        self.act(d2[:], d2[:], AF.Exp, [D2], [D2])
        eg, EG = self.r_f128.next()
        self.act(eg[:], pg1, AF.Exp, [PG1], [EG])
        Bm, BM = self.r_b128.next()
        self.tt("vector", Bm[:], pG, d2[:], ALU.mult, [PGG, D2], [BM])
        qkd, QKD = self.r_b128.next()
        self.tt("vector", qkd[:], pQ, d1[:], ALU.mult, [PQQ, D1], [QKD])
        qdec, QDEC = self.r_b128.next()
        self.tt("vector", qdec[:], qT, eg[:], ALU.mult, [QKVZ, EG], [QDEC])
        kbg, KBG = self.r_b128.next()
        self.act(kbg[:], pkt, AF.Copy, [PKT, G], [KBG], scale=col(3))
        ktl, KTL = self.r_b128.next()
        self.ts("vector", ktl[:], pkt, col(4), ALU.mult, [PKT, G], [KTL])
        vb, VB = self.r_b128.next()
        self.act(vb[:], pvt, AF.Copy, [PVT, G], [VB], scale=col(5))
        import os
        ks = os.environ.get("KSTOP", "full")
        if ks.endswith("g1"):
            return
        pa_, PA = self.r_small.next()
        pat = pa_.bitcast(BF16)[:, 0:128]
        self.tr(pat, Bm[:], self.identb[:], [BM, C], [PA])
        Am, AM = self.r_b128.next()
        self.cp("scalar", Am[:], pat, [PA], [AM])
        Pm, PM = self.r_b128.next()
        self.tt("vector", Pm[:], self.identb[:], Bm[:], ALU.subtract, [C, BM], [PM])
        nlev = 2 if samp else 5
        Ak, AK, Bk, BK = Am, AM, Bm, BM
        for lev in range(nlev):
            pa2, PA2 = self.r_small.next()
            self.mm(pa2, Bk[:], Ak[:], True, True, [BK, AK], [PA2])
            An, AN = self.r_b128.next()
            self.cp("scalar", An[:], pa2, [PA2], [AN])
            if lev < nlev - 1:
                pb2, PB2 = self.r_small.next()
                self.mm(pb2, Ak[:], Bk[:], True, True, [BK, AK], [PB2])
                Bn, BN = self.r_b128.next()
                self.cp("vector", Bn[:], pb2, [PB2], [BN])
            pp, PP = self.r_small.next()
            self.mm(pp, An[:], Pm[:], True, True, [AN, PM], [PP])
            Pn, PN = self.r_b128.next()
            self.tt("vector", Pn[:], pp, Pm[:], ALU.add, [PP, PM], [PN])
            Pm, PM = Pn, PN
            Ak, AK = An, AN
            if lev < nlev - 1:
                Bk, BK = Bn, BN
        TT, TTB = Pm, PM
        if ks.endswith("g2"):
            return
        pw, PW = self.r_small.next()
        self.mm(pw, kbg[:], TT[:], True, True, [KBG, TTB], [PW])
        vn, VN = self.r_b128.next()
        po_i, POI = self.r_small.next()
        if not samp:
            nw, NW = self.r_b128.next()
            self.act(nw[:], pw, AF.Copy, [PW], [NW], scale=-1.0)
            for c in range(2):
                sl = slice(64 * c, 64 * c + 64)
                self.mm(po_i[:, sl], Sb_[:], qdec[:, sl], True, True, [SBF, QDEC], [POI])
                pvn, PVN = self.r_small.next()
                self.mm(pvn[sl, :], TT[:, sl], vb[:], True, False, [TTB, VB], [PVN])
                self.mm(pvn[sl, :], nw[:, sl], Sb_[:], False, True, [NW, SBF], [PVN])
                self.cp("scalar", vn[sl, :], pvn[sl, :], [PVN], [VN])
                pS, PSS = self.r_small.next()
                self.mm(pS, ktl[sl, :], vn[sl, :], True, True, [KTL, VN], [PSS])
                self.stt(S_[:], S_[:], eg[:, 64 * c + 63:64 * c + 64], pS, ALU.mult, ALU.add, [SBF, EG, PSS], [SBF])
                self.cp("scalar", Sb_[:], S_[:], [SBF], [SBF])
        else:
            SS = self.SSAMP
            self.dma("sync", self.sem_ss, self.ssamp[:], self.sgdn[l, :, h].rearrange("s k v -> k s v"), [], [SS])
            self.cp("vector", self.ssampb[:], self.ssamp[:], [SS], [SS])
            wzd = self.wz[:].rearrange("p s i -> p (s i)")
            dst = bass.AP(tensor=wzd.tensor, offset=wzd.offset, ap=[list(wzd.ap[0]), [136, NSEQ], [1, LS]])
            self.act(dst, pw.rearrange("p (s j) -> p s j", j=LS), AF.Copy, [PW], [self.WZ], scale=-1.0)
            bm16 = self.cmat[:, 9 * 128:9 * 128 + 16]
            self.tt("vector", self.ktz[:], ktl[:].unsqueeze(1).broadcast_to([128, NSEQ, 128]),
                    bm16.unsqueeze(2).broadcast_to([128, NSEQ, 128]), ALU.mult, [KTL, C], [self.KTZ])
            for s in range(NSEQ):
                self.mm(po_i[:, s * LS:(s + 1) * LS], self.ssampb[:, s, :], qdec[:, s * LS:(s + 1) * LS], True, True,
                        [SS, QDEC], [POI])
            oi, OI = self.r_f128.next()
            self.cp("scalar", oi[:], po_i, [POI], [OI])
            pvn, PVN = self.r_small.next()
            self.mm(pvn, TT[:], vb[:], True, False, [TTB, VB], [PVN])
            for s in range(NSEQ):
                self.mm(pvn, self.wz[:, s, :], self.ssampb[:, s, :], False, s == NSEQ - 1, [self.WZ, SS], [PVN])
            self.cp("scalar", vn[:], pvn, [PVN], [VN])
            for s in range(NSEQ):
                pS, PSS = self.r_small.next()
                self.mm(pS, self.ktz[:, s, :], vn[:], True, True, [self.KTZ, VN], [PSS])
                self.stt(self.ssamp[:, s, :], self.ssamp[:, s, :], eg[:, s * LS + LS - 1:s * LS + LS], pS,
                         ALU.mult, ALU.add, [SS, EG, PSS], [SS])
            self.dma("sync", self.sem_ss, self.o_ss[l, :, h].rearrange("s k v -> k s v"), self.ssamp[:], [SS], [])
        if ks.endswith("g3"):
            return
        po, PO = self.r_small.next()
        self.mm(po, vn[:], qkd[:], True, True, [VN, QKD], [PO])
        if not samp:
            oi, OI = self.r_f128.next()
            self.cp("scalar", oi[:], po_i, [POI], [OI])
        self.tt("vector", oi[:], po, oi[:], ALU.add, [PO, OI], [OI])
        sq, SQ = self.r_b128.next()
        self.act(sq[:], oi[:], AF.Square, [OI], [SQ])
        pss, PSS2 = self.r_small.next()
        self.mm(pss, self.onesb[:], sq[:], True, True, [SQ, C], [PSS2])
        rn, RN = self.r_f128.next()
        self.act(rn[:], pss, AF.Ln, [PSS2, C], [RN], scale=1.0 / 128, bias=self.cst(0))
        self.act(rn[:], rn[:], AF.Exp, [RN], [RN], scale=-0.5)
        self.stt(oi[:], oi[:], self.ggdn(l), rn[:], ALU.mult, ALU.mult, [OI, RN, C], [OI])
        self.tt("vector", self.xn[:, h, c0:c0 + 128], oi[:], zs, ALU.mult, [OI, QKVZ], [self.XN[gi]])

    def attn(self, l, bi, blk):
        TBM = self.TBM
        C = self.CONST
        self.norm(l, 2, blk)
        ov = self.ovl
        qT = ov[:, 0:8 * TBM].rearrange("p (c n) -> p c n", n=TBM)
        aoT = ov[:, 8 * TBM:16 * TBM].rearrange("p (c n) -> p c n", n=TBM)
        QT, AO, self.KV = self.phase(["qT", "aoT", "kv"])
        self.mem_kv(l, bi == 0)
        for s4 in range(4):
            sl, SL = self.slab("xq", l, 0, 8, [(s4 * 256, 256)])
            for i in range(2):
                for gi, (t0, n, kind, lo) in enumerate(blk):
                    ps, PB = self.proj(sl, SL, 8, i * 128, self.xn_in, self.XN[gi], gi, lo, n)
                    self.cp("scalar", qT[:, s4 * 2 + i, lo:lo + n], ps[:, :n], [PB], [QT])
        for gi, (t0, n, kind, lo) in enumerate(blk):
            if kind == 'p':
                for h in range(4):
                    self.attn_core(self.KT, self.Vb, self.KV, qT, QT, aoT, AO, h, lo, n)
            else:
                KVSB = self.sphase(["kvs0", "kvs1", "kts"])
                kts, KTS = self.kts_t, KVSB[2]
                for s in range(NSEQ):
                    k_ = s % 2
                    kv, KVS = self.kvs_t[k_], KVSB[k_]
                    self.dma("gpsimd", self.kvs_sems[k_], kv[:, 0], self.cK[l, s].rearrange("(t p) d -> p t d", p=128), [], [KVS])
                    self.dma("gpsimd", self.kvs_sems[k_], kv[:, 1], self.cV[l, s].rearrange("(t p) d -> p t d", p=128), [], [KVS])
                    for mt in range(2):
                        for half in range(2):
                            pb_, PB = self.r_big4.next()
                            pbt = pb_.bitcast(BF16)[:, 0:512]
                            for c4 in range(4):
                                c = half * 4 + c4
                                self.tr(pbt[:, c4 * 128:(c4 + 1) * 128], kv[:, 0, mt, c * 128:(c + 1) * 128], self.identb[:],
                                        [KVS, C], [PB])
                            self.cp("vector" if half else "scalar", kts[:, half * 4:half * 4 + 4, mt * 128:(mt + 1) * 128],
                                    pbt.rearrange("p (c n) -> p c n", n=128), [PB], [KTS])
                    gens = [self.attn_core_g(kts, kv[:, 1], [KTS, KVS], qT, QT, aoT, AO, h, lo + s * LS, LS) for h in range(4)]
                    while gens:
                        for g_ in list(gens):
                            try:
                                next(g_)
                            except StopIteration:
                                gens.remove(g_)
        for s4 in range(4):
            sl, SL = self.slab("xo", l, 0, 8, [(s4 * 256, 256)])
            for i in range(2):
                dc = s4 * 2 + i
                for gi, (t0, n, kind, lo) in enumerate(blk):
                    ps, PB = self.r_big.next()
                    for kc in range(8):
                        self.mm(ps[:, :n], sl[:, kc, i * 128:(i + 1) * 128], aoT[:, kc, lo:lo + n], kc == 0, kc == 7,
                                [SL, AO], [PB])
                    X = self.XT[t0]
                    self.tt("vector", self.xT[:, dc, t0:t0 + n], ps[:, :n], self.xT[:, dc, t0:t0 + n], ALU.add, [PB, X], [X])

    def attn_core_g(self, KT, V, rd, qT, QT, aoT, AO, h, lo, n):
        C = self.CONST
        pss, ex, pos = [], [], []
        for mt in range(2):
            ps, PB = self.r_small.next()
            for dc in range(2):
                self.mm(ps[:, :n], KT[:, 2 * h + dc, mt * 128:(mt + 1) * 128], qT[:, 2 * h + dc, lo:lo + n], dc == 0, dc == 1,
                        rd + [QT], [PB])
            pss.append((ps, PB))
        yield
        for mt in range(2):
            e, E = self.r_b128.next()
            self.act(e[:, :n], pss[mt][0][:, :n], AF.Exp, [pss[mt][1]], [E], scale=1.0 / 16.0)
            ex.append((e, E))
        yield
        pd, PD = self.r_small.next()
        for mt in range(2):
            self.mm(pd[:, :n], self.onesb[:], ex[mt][0][:, :n], mt == 0, mt == 1, [ex[mt][1], C], [PD])
        for dc in range(2):
            po, PO = self.r_small.next()
            for mt in range(2):
                self.mm(po[:, :n], V[:, mt, (2 * h + dc) * 128:(2 * h + dc + 1) * 128], ex[mt][0][:, :n], mt == 0, mt == 1,
                        rd + [ex[mt][1]], [PO])
            pos.append((po, PO))
        yield
        rd_, RD = self.r_f128.next()
        self.act(rd_[:, :n], pd[:, :n], AF.Ln, [PD], [RD])
        self.act(rd_[:, :n], rd_[:, :n], AF.Exp, [RD], [RD], scale=-1.0)
        yield
        for dc in range(2):
            self.tt("vector", aoT[:, 2 * h + dc, lo:lo + n], pos[dc][0][:, :n], rd_[:, :n], ALU.mult, [pos[dc][1], RD], [AO])

    def attn_core(self, KT, V, KVB, qT, QT, aoT, AO, h, lo, n, KVB2=None):
        C = self.CONST
        rd = [KVB] + ([KVB2] if KVB2 is not None else [])
        ex = []
        for mt in range(2):
            ps, PB = self.r_big.next()
            for dc in range(2):
                self.mm(ps[:, :n], KT[:, 2 * h + dc, mt * 128:(mt + 1) * 128], qT[:, 2 * h + dc, lo:lo + n], dc == 0, dc == 1,
                        rd + [QT], [PB])
            e, E = self.r_b512.next()
            self.act(e[:, :n], ps[:, :n], AF.Exp, [PB], [E], scale=1.0 / 16.0)
            ex.append((e, E))
        pd, PD = self.r_big.next()
        for mt in range(2):
            self.mm(pd[:, :n], self.onesb[:], ex[mt][0][:, :n], mt == 0, mt == 1, [ex[mt][1], C], [PD])
        rd_, RD = self.r_f512.next()
        self.act(rd_[:, :n], pd[:, :n], AF.Ln, [PD], [RD])
        self.act(rd_[:, :n], rd_[:, :n], AF.Exp, [RD], [RD], scale=-1.0)
        for dc in range(2):
            po, PO = self.r_big.next()
            for mt in range(2):
                self.mm(po[:, :n], V[:, mt, (2 * h + dc) * 128:(2 * h + dc + 1) * 128], ex[mt][0][:, :n], mt == 0, mt == 1,
                        rd + [ex[mt][1]], [PO])
            self.tt("vector", aoT[:, 2 * h + dc, lo:lo + n], po[:, :n], rd_[:, :n], ALU.mult, [PO, RD], [AO])

    def mem_kv(self, l, emit_out):
        C = self.CONST
        memin = lambda kc, lo, n: self.memT[:, kc, lo:lo + n]
        for which in range(2):
            wn = "xk" if which == 0 else "xv"
            outd = self.o_mk if which == 0 else self.o_mv
            for s4 in range(4):
                sl, SL = self.slab(wn, l, 0, 8, [(s4 * 256, 256)])
                if which == 0:
                    for i in range(2):
                        ps, PB = self.proj(sl, SL, 8, i * 128, memin, self.MEMT, 0, 0, MEM)
                        self.cp("scalar", self.KT[:, s4 * 2 + i, :], ps[:, :MEM], [PB], [self.KV])
                for mt in range(2):
                    ps, PB = self.r_big.next()
                    for kc in range(8):
                        self.mm(ps[:, :256], self.memT[:, kc, mt * 128:(mt + 1) * 128], sl[:, kc, :], kc == 0, kc == 7,
                                [self.MEMT, SL], [PB])
                    if which == 1:
                        self.cp("scalar", self.Vb[:, mt, s4 * 256:(s4 + 1) * 256], ps[:, :256], [PB], [self.KV])
                    if emit_out:
                        st, SB_ = self.r_stage.next()
                        k = 0
                        self.cp("vector", st[:, 0:256], ps[:, :256], [PB], [SB_])
                        self.dma("sync", self.stage_sems[k], outd[l, mt * 128:(mt + 1) * 128, s4 * 256:(s4 + 1) * 256],
                                 st[:, 0:256], [SB_], [])

    def final(self):
        C = self.CONST
        identf = self.cm(0)
        for blk in self.blocks:
            for gi, (t0, n, kind, lo) in enumerate(blk):
                X = self.XT[t0]
                rs, RS = self.sumsq(X, t0, n)
                for c in range(8):
                    self.stt(self.xT[:, c, t0:t0 + n], self.xT[:, c, t0:t0 + n], self.gfin(c), rs[:, :n],
                             ALU.mult, ALU.mult, [X, RS, C], [X])
                for t_ in range(n // 128):
                    tok = t0 + t_ * 128
                    st, SB_ = self.r_stage.next()
                    k = 0
                    for half in range(2):
                        pb_, PB2 = self.r_big.next()
                        for c4 in range(4):
                            c = half * 4 + c4
                            self.tr(pb_[:, c4 * 128:(c4 + 1) * 128], self.xT[:, c, tok:tok + 128], identf, [X, C], [PB2])
                        self.cp("vector" if half else "scalar", st[:, half * 512:(half + 1) * 512], pb_[:], [PB2], [SB_])
                    dst = self.yp[tok:tok + 128, :] if kind == 'p' else self.ys[:, :]
                    self.dma("sync", self.stage_sems[k], dst, st[:], [SB_], [])


def cs_view(cs, n, kind):
    return cs[:, :n] if kind == 'p' else cs[:, :n].rearrange("p (s t) -> p s t", t=LS)


def ph_view(ph, n, kind):
    return ph[:, :n] if kind == 'p' else ph[:, :n].rearrange("p (s t) -> p s t", t=LS)


ps_view = ph_view


def host_consts(L, g_ffn1, g_mix, g_xattn, g_ffn2, g_final, conv_qkv_w, sconv_w, g_gdn_out, a_log, dt_bias):
    f = np.float32
    fm = lambda v: np.ascontiguousarray(np.asarray(v, f).reshape(-1, 128).T)
    NV = L * 32 + 8 + L * 48 + L * 12 + L + 4
    cvec = np.zeros((128, NV), f)
    for l in range(L):
        for wi, g in enumerate((g_ffn1, g_mix, g_xattn, g_ffn2)):
            cvec[:, (l * 4 + wi) * 8:(l * 4 + wi) * 8 + 8] = fm(g[l])
    o = L * 32
    cvec[:, o:o + 8] = fm(g_final)
    o += 8
    for l in range(L):
        for c in range(12):
            cvec[:, o + (l * 12 + c) * 4:o + (l * 12 + c) * 4 + 4] = np.asarray(conv_qkv_w[l], f)[:, c * 128:(c + 1) * 128].T
    o += L * 48
    for l in range(L):
        for c in range(4):
            cvec[:, o + (l * 4 + c) * 3:o + (l * 4 + c) * 3 + 3] = np.asarray(sconv_w[l], f)[:, c * 128:(c + 1) * 128].T
    o += L * 12
    for l in range(L):
        cvec[:, o + l] = np.asarray(g_gdn_out[l], f)
    o += L
    cvec[:, o] = 1e-6
    cvec[:, o + 1] = 1.0
    crow = np.zeros((128, 2 * L * 24), f)
    for l in range(L):
        crow[:, l * 24:(l + 1) * 24] = np.tile(np.asarray(dt_bias[l], f), 6)[None, :]
        crow[:, L * 24 + l * 24:L * 24 + (l + 1) * 24] = np.tile(np.asarray(a_log[l], f), 6)[None, :]
    cmat = np.zeros((128, 9 * 128 + 16), f)
    j = np.arange(128)[:, None]
    i = np.arange(128)[None, :]
    cmat[:, 0:128] = (i == j)
    for k, bs in ((0, 64), (1, 8)):
        same = (i // bs) == (j // bs)
        cmat[:, (1 + 2 * k) * 128:(2 + 2 * k) * 128] = same & (j <= i)
        cmat[:, (2 + 2 * k) * 128:(3 + 2 * k) * 128] = same
        cmat[:, (5 + 2 * k) * 128:(6 + 2 * k) * 128] = np.where(same & (i >= j), 0.0, NEG)
        cmat[:, (6 + 2 * k) * 128:(7 + 2 * k) * 128] = np.where(same & (i > j), 0.0, NEG)
    cmat[:, 9 * 128:9 * 128 + 16] = (j // 8) == np.arange(16)[None, :]
    return cvec, crow, cmat


_CACHE = {}


def get_program(L, SEQ):
    key = (L, SEQ)
    if key not in _CACHE:
        d = Builder(bass.Bass("TRN2", target_bir_lowering=False), L, SEQ, dry=True)
        d.build()
        nc = bass.Bass("TRN2", target_bir_lowering=False)
        b = Builder(nc, L, SEQ, dry=False, log=d.slab_log)
        b.build()
        _CACHE[key] = nc
    return _CACHE[key]


def kernel(x_prompt, x_sample, mem_prompt, state_gdn, state_qkv_conv, state_short_conv,
           cache_mem_k, cache_mem_v, g_ffn1, w_ffn1_gu, w_ffn1_down, g_mix, w_in, conv_qkv_w,
           a_log, dt_bias, g_gdn_out, sconv_w, w_out, g_xattn, w_xq, w_xk, w_xv, w_xo,
           g_ffn2, w_ffn2_gu, w_ffn2_down, g_final, _ncores=8, _runner=None):
    f = np.float32
    A = lambda v: np.ascontiguousarray(np.asarray(v, dtype=f))
    L = int(np.asarray(w_in).shape[0])
    SEQ = int(np.asarray(x_prompt).shape[1])
    NC_ = _ncores
    nc = get_program(L, SEQ)
    cvec, crow, cmat = host_consts(L, g_ffn1, g_mix, g_xattn, g_ffn2, g_final, conv_qkv_w, sconv_w,
                                   g_gdn_out, a_log, dt_bias)
    xpr, xsa, mem = A(x_prompt), A(x_sample), A(mem_prompt)
    sg, sq, sc = A(state_gdn), A(state_qkv_conv), A(state_short_conv)
    ck, cv = A(cache_mem_k), A(cache_mem_v)
    shared = {"w_ffn1_gu": A(w_ffn1_gu), "w_ffn1_down": A(w_ffn1_down), "w_in": A(w_in), "w_out": A(w_out),
              "w_xq": A(w_xq), "w_xk": A(w_xk), "w_xv": A(w_xv), "w_xo": A(w_xo),
              "w_ffn2_gu": A(w_ffn2_gu), "w_ffn2_down": A(w_ffn2_down), "cvec": cvec, "crow": crow, "cmat": cmat}
    in_maps = []
    for c in range(NC_):
        b0, b1 = c * NSEQ, (c + 1) * NSEQ
        m = dict(shared)
        m["xp"] = xpr[c]
        m["xs"] = np.ascontiguousarray(xsa[b0:b1].reshape(NSEQ * LS, D))
        m["mem"] = mem[c]
        m["sgdn"] = np.ascontiguousarray(sg[:, b0:b1])
        m["sqkv"] = np.ascontiguousarray(sq[:, b0:b1].reshape(L, NSEQ * 3, 1536))
        m["ssc"] = np.ascontiguousarray(sc[:, b0:b1].reshape(L, NSEQ * 2, 512))
        m["cK"] = np.ascontiguousarray(ck[:, b0:b1].reshape(L, NSEQ, MEM, D))
        m["cV"] = np.ascontiguousarray(cv[:, b0:b1].reshape(L, NSEQ, MEM, D))
        in_maps.append(m)
    if _runner is None:
        res = run_bass_kernel_spmd(nc, in_maps, core_ids=list(range(NC_))).results
    else:
        res = _runner(nc, in_maps)
    st = lambda k: np.stack([np.asarray(r[k], f) for r in res], axis=0)
    yp = st("yp")
    ys = st("ys").reshape(NC_ * NSEQ, LS, D)
    p_s = np.ascontiguousarray(st("o_sp").transpose(1, 0, 2, 3, 4))
    p_q = np.ascontiguousarray(st("o_qp").reshape(NC_, L, 3, 1536).transpose(1, 0, 2, 3))
    p_c = np.ascontiguousarray(st("o_cp").reshape(NC_, L, 2, 512).transpose(1, 0, 2, 3))
    p_mk = np.ascontiguousarray(st("o_mk").reshape(NC_, L, MEM, 4, 256).transpose(1, 0, 2, 3, 4))
    p_mv = np.ascontiguousarray(st("o_mv").reshape(NC_, L, MEM, 4, 256).transpose(1, 0, 2, 3, 4))
    s_s = np.ascontiguousarray(st("o_ss").transpose(1, 0, 2, 3, 4, 5)).reshape(L, NC_ * NSEQ, 4, 128, 128)
    s_q = np.ascontiguousarray(st("o_qs").reshape(NC_, L, NSEQ, 3, 1536).transpose(1, 0, 2, 3, 4)).reshape(L, NC_ * NSEQ, 3, 1536)
    s_c = np.ascontiguousarray(st("o_cs").reshape(NC_, L, NSEQ, 2, 512).transpose(1, 0, 2, 3, 4)).reshape(L, NC_ * NSEQ, 2, 512)
    return (yp, ys, p_s, p_q, p_c, p_mk, p_mv, s_s, s_q, s_c)
```
